# Optimizing a Trainium2 kernel written in Bass

```python
import math
import jax, jax.numpy as jnp
from jax import lax
import numpy as np

D_MODEL = 4096
BATCH = 1
SEQ = 8192
DEPTH = 1

PLE_DIM = 256
N_ATT_HEADS = 8
ATT_HEAD_DIM = 128
ATT_V_DIM = 2 * ATT_HEAD_DIM
ATT_WIDTH = N_ATT_HEADS * ATT_V_DIM
Q_BLOCK = 128
CONV_WIDTH = D_MODEL // 2
CONV_KERNEL = 31
Q_COLS = N_ATT_HEADS * 2 * ATT_HEAD_DIM
K_COLS = N_ATT_HEADS * 2 * ATT_HEAD_DIM
V_COLS = N_ATT_HEADS * ATT_V_DIM
ZA_COLS = ATT_WIDTH
GLU_COLS = 2 * CONV_WIDTH
ZC_COLS = CONV_WIDTH
GATE_COLS = 2 * D_MODEL
_SIZES = [Q_COLS, K_COLS, V_COLS, ZA_COLS, GLU_COLS, ZC_COLS, D_MODEL, D_MODEL]
IN_COLS = sum(_SIZES)
SPLIT_POINTS = [int(v) for v in np.cumsum(_SIZES)[:-1]]

RMS_EPS = 1e-6
SUBLN_EPS = 1e-5
LN_EPS = 1e-5

kernel_name = 'hybrid_diffattn_conformer_gated_block'


def rms_norm(x, g, eps=RMS_EPS):
    xf = x.astype(jnp.float32)
    y = xf * lax.rsqrt(jnp.mean(xf * xf, axis=-1, keepdims=True) + eps)
    return (y * g.astype(jnp.float32)).astype(x.dtype)


def layer_norm(x, g, b, eps=LN_EPS):
    xf = x.astype(jnp.float32)
    mu = jnp.mean(xf, axis=-1, keepdims=True)
    xc = xf - mu
    y = xc * lax.rsqrt(jnp.mean(xc * xc, axis=-1, keepdims=True) + eps)
    return (y * g.astype(jnp.float32) + b.astype(jnp.float32)).astype(x.dtype)


def diff_attention(q, k, v, lam):
    b, s = q.shape[0], q.shape[1]
    nb = s // Q_BLOCK
    qb = q.reshape(b, nb, Q_BLOCK, N_ATT_HEADS, 2, ATT_HEAD_DIM).transpose(1, 0, 2, 3, 4, 5)
    scale = ATT_HEAD_DIM ** -0.5
    k_pos = jnp.arange(s)

    def block(args):
        q_blk, i = args
        sc = jnp.einsum('bqhcd,bkhcd->bhcqk', q_blk, k, preferred_element_type=jnp.float32) * scale
        q_pos = i * Q_BLOCK + jnp.arange(Q_BLOCK)
        causal = k_pos[None, :] <= q_pos[:, None]
        sc = jnp.where(causal, sc, -jnp.inf)
        pr = jax.nn.softmax(sc, axis=-1)
        att = pr[:, :, 0] - lam * pr[:, :, 1]
        return jnp.einsum('bhqk,bkhd->bqhd', att.astype(v.dtype), v)

    out = lax.map(block, (qb, jnp.arange(nb)))
    return out.transpose(1, 0, 2, 3, 4).reshape(b, s, N_ATT_HEADS, ATT_V_DIM)


def causal_depthwise_conv(u, w, bias):
    y = lax.conv_general_dilated(
        u, w[:, None, :].astype(u.dtype), window_strides=(1,),
        padding=[(CONV_KERNEL - 1, 0)],
        dimension_numbers=('NWC', 'WIO', 'NWC'),
        feature_group_count=u.shape[-1])
    return y + bias


def setup_inputs(seed: int = 0) -> dict:
    key = jax.random.key(seed)
    ks = jax.random.split(key, 24)
    f32 = jnp.float32

    def nrm(k, shape, std):
        return jax.random.normal(k, shape, f32) * std

    def gain(k, shape):
        return 1.0 + 0.02 * jax.random.normal(k, shape, f32)

    return {
        'x': jax.random.normal(ks[0], (BATCH, SEQ, D_MODEL), f32),
        'p': jax.random.normal(ks[1], (DEPTH, BATCH, SEQ, PLE_DIM), f32),
        'g_mix': gain(ks[2], (DEPTH, D_MODEL)),
        'w_in': nrm(ks[3], (DEPTH, D_MODEL, IN_COLS), D_MODEL ** -0.5),
        'lambda_q1': nrm(ks[4], (DEPTH, ATT_HEAD_DIM), 0.1),
        'lambda_k1': nrm(ks[5], (DEPTH, ATT_HEAD_DIM), 0.1),
        'lambda_q2': nrm(ks[6], (DEPTH, ATT_HEAD_DIM), 0.1),
        'lambda_k2': nrm(ks[7], (DEPTH, ATT_HEAD_DIM), 0.1),
        'g_subln': gain(ks[8], (DEPTH, ATT_V_DIM)),
        'w_att_out': nrm(ks[9], (DEPTH, ATT_WIDTH, D_MODEL), ATT_WIDTH ** -0.5),
        'conv_w': nrm(ks[10], (DEPTH, CONV_KERNEL, CONV_WIDTH), CONV_KERNEL ** -0.5),
        'conv_b': nrm(ks[11], (DEPTH, CONV_WIDTH), 0.02),
        'ln_g': gain(ks[12], (DEPTH, CONV_WIDTH)),
        'ln_b': nrm(ks[13], (DEPTH, CONV_WIDTH), 0.02),
        'w_conv_out': nrm(ks[14], (DEPTH, CONV_WIDTH, D_MODEL), CONV_WIDTH ** -0.5),
        'w_out': nrm(ks[15], (DEPTH, D_MODEL, D_MODEL), D_MODEL ** -0.5),
        'g_ple_gate': gain(ks[16], (DEPTH, D_MODEL)),
        'w_ple_gate': nrm(ks[17], (DEPTH, D_MODEL, D_MODEL), D_MODEL ** -0.5),
        'w_ple': nrm(ks[18], (DEPTH, PLE_DIM, D_MODEL), PLE_DIM ** -0.5),
        'g_ple_post': gain(ks[19], (DEPTH, D_MODEL)),
        'g_final': gain(ks[20], (D_MODEL,)),
    }


def reference(x, p, g_mix, w_in, lambda_q1, lambda_k1, lambda_q2, lambda_k2, g_subln,
              w_att_out, conv_w, conv_b, ln_g, ln_b, w_conv_out, w_out,
              g_ple_gate, w_ple_gate, w_ple, g_ple_post, g_final):
    b, s, _ = x.shape
    for i in range(DEPTH):
        lambda_init = 0.8 - 0.6 * math.exp(-0.3 * i)
        h = rms_norm(x, g_mix[i])
        proj = h @ w_in[i]
        q, k, v, z_a, glu, z_c, gate_a, gate_c = jnp.split(proj, SPLIT_POINTS, axis=-1)

        q = q.reshape(b, s, N_ATT_HEADS, 2, ATT_HEAD_DIM)
        k = k.reshape(b, s, N_ATT_HEADS, 2, ATT_HEAD_DIM)
        v = v.reshape(b, s, N_ATT_HEADS, ATT_V_DIM)
        lam = (jnp.exp(jnp.sum(lambda_q1[i].astype(jnp.float32) * lambda_k1[i].astype(jnp.float32)))
               - jnp.exp(jnp.sum(lambda_q2[i].astype(jnp.float32) * lambda_k2[i].astype(jnp.float32)))
               + lambda_init)
        o = diff_attention(q, k, v, lam)
        o = rms_norm(o, g_subln[i], SUBLN_EPS) * (1.0 - lambda_init)
        y_a = (o.reshape(b, s, ATT_WIDTH) * jax.nn.silu(z_a)) @ w_att_out[i]

        ga, gb = jnp.split(glu, 2, axis=-1)
        u = ga * jax.nn.sigmoid(gb)
        u = causal_depthwise_conv(u, conv_w[i], conv_b[i])
        u = jax.nn.silu(layer_norm(u, ln_g[i], ln_b[i]))
        y_c = (u * jax.nn.silu(z_c)) @ w_conv_out[i]

        m = jax.nn.sigmoid(gate_a) * y_a + jax.nn.sigmoid(gate_c) * y_c
        x = x + m @ w_out[i]

        e = rms_norm(p[i] @ w_ple[i], g_ple_post[i])
        g = jax.nn.sigmoid(rms_norm(x, g_ple_gate[i]) @ w_ple_gate[i])
        x = x + g * e
    return rms_norm(x, g_final)
```

```python
import contextlib
import numpy as np
import ml_dtypes
import concourse.bass as bass
import concourse.mybir as mybir
from concourse.bass_utils import run_bass_kernel_spmd

F32 = mybir.dt.float32
BF16 = mybir.dt.bfloat16
AF = mybir.ActivationFunctionType
ALU = mybir.AluOpType
AX = mybir.AxisListType

FUSED = True

NCORES = 8
S = 8192
D = 4096
KC = 32
TC = 1024
TB = 512
NB = 2
HALO = 32
C_ZA = 6144
C_GA = 8192
C_GB = 10240
C_ZC = 12288
C_GTA = 14336
C_GTC = 18432
NTAP = 31
LAMBDA_INIT = 0.2
RMS_EPS = 1e-6
SUBLN_EPS = 1e-5
LN_EPS = 1e-5

V_GMIX = 0
V_GPG = 32
V_GPP = 64
V_GFIN = 96
V_CB = 128
V_LNG = 144
V_LNB = 160
V_CW = 176
V_SEL = V_CW + 16 * NTAP
V_LQ1 = V_SEL + 8
V_LK1 = V_LQ1 + 128
V_LQ2 = V_LK1 + 128
V_LK2 = V_LQ2 + 128
V_GSUB = V_LK2 + 128
NV = V_GSUB + 256


class Sched:
    def __init__(self):
        self.ops = []
        self.last_w = {}
        self.readers = {}
        self.dma_count = {}
        self.group_all = set()

    def add(self, eng, fn, reads=(), writes=(), dma_key=None, group_all=False):
        idx = len(self.ops)
        deps = set()
        for k in reads:
            w = self.last_w.get(k)
            if w is not None:
                deps.add(w)
        for k in writes:
            w = self.last_w.get(k)
            if w is not None:
                deps.add(w)
            for r in self.readers.get(k, {}).values():
                deps.add(r)
        if dma_key is None and eng == "pe":
            deps = {d for d in deps if not (self.ops[d]["eng"] == "pe" and self.ops[d]["dma_key"] is None)}
        op = dict(eng=eng, fn=fn, deps=sorted(deps), dma_key=dma_key, signal=False, tick=None)
        if dma_key is not None:
            n = self.dma_count.get(dma_key, 0) + 1
            self.dma_count[dma_key] = n
            op["dma_ord"] = n
            if group_all:
                self.group_all.add(dma_key)
        self.ops.append(op)
        rid = eng if dma_key is None else ("dma", idx)
        for k in reads:
            self.readers.setdefault(k, {})[rid] = idx
        for k in writes:
            self.last_w[k] = idx
            self.readers[k] = {}
        return idx

    def emit(self, nc, stack, final_keys=()):
        ops = self.ops
        for op in ops:
            for d in op["deps"]:
                ops[d]["signal"] = True
        engs = ["pe", "act", "dve", "pool", "sp"]
        esem = {e: stack.enter_context(nc.semaphore("s_" + e)) for e in engs}
        dsem = {}
        for i, k in enumerate(self.dma_count):
            dsem[k] = stack.enter_context(nc.semaphore("d%d" % i))
        cnt = {e: 0 for e in engs}
        for op in ops:
            if op["dma_key"] is not None:
                k = op["dma_key"]
                if k in self.group_all:
                    op["tick"] = (dsem[k], 16 * self.dma_count[k])
                else:
                    op["tick"] = (dsem[k], 16 * op["dma_ord"])
            elif op["signal"]:
                cnt[op["eng"]] += 1
                op["tick"] = (esem[op["eng"]], cnt[op["eng"]])
        per = {e: [] for e in engs}
        for op in ops:
            per[op["eng"]].append(op)
        finals = [(dsem[k], 16 * self.dma_count[k]) for k in final_keys if k in dsem]

        def run(e, name):
            known = {}
            for op in per[name]:
                for d in op["deps"]:
                    sem, val = ops[d]["tick"]
                    if known.get(id(sem), 0) >= val:
                        continue
                    e.wait_ge(sem, val)
                    known[id(sem)] = val
                if op["fn"] is None:
                    assert not op["signal"]
                    continue
                ins = op["fn"](e)
                if op["dma_key"] is not None:
                    ins.then_inc(op["tick"][0], 16)
                elif op["signal"]:
                    ins.then_inc(op["tick"][0], 1)
            if name == "sp":
                for sem, val in finals:
                    e.wait_ge(sem, val)

        with nc.Block() as block:
            @block.sync
            def _(e):
                run(e, "sp")

            @block.tensor
            def _(e):
                run(e, "pe")

            @block.scalar
            def _(e):
                run(e, "act")

            @block.vector
            def _(e):
                run(e, "dve")

            @block.gpsimd
            def _(e):
                run(e, "pool")


class Builder:
    def __init__(self, nc, stack, sc):
        self.nc = nc
        self.stack = stack
        self.sc = sc
        self.rr = {}

    def sb(self, name, shape, dt, stack=None):
        return (stack or self.stack).enter_context(self.nc.sbuf_tensor(name, shape, dt))

    def nxt(self, name, n):
        i = self.rr.get(name, 0)
        self.rr[name] = i + 1
        return i % n

    def dma(self, out, in_, reads, writes, key, group_all=False):
        self.sc.add("sp", lambda e, o=out, i=in_: e.dma_start(out=o, in_=i), reads, writes,
                    dma_key=key, group_all=group_all)

    def mm(self, out, lhsT, rhs, start, stop, reads, writes, skip=False):
        if skip:
            fn = lambda e, o=out, l=lhsT, r=rhs, a=start, b=stop: e.matmul(o, l, r, start=a, stop=b, skip_group_check=True)
        else:
            fn = lambda e, o=out, l=lhsT, r=rhs, a=start, b=stop: e.matmul(o, l, r, start=a, stop=b)
        self.sc.add("pe", fn, reads, writes)

    def tr(self, out, in_, ident, reads, writes):
        self.sc.add("pe", lambda e, o=out, i=in_, d=ident: e.transpose(o, i, d), reads, writes)

    def act(self, out, in_, func, reads, writes, bias=None, scale=None):
        kw = {}
        if bias is not None:
            kw["bias"] = bias
        if scale is not None:
            kw["scale"] = scale
        self.sc.add("act", lambda e, o=out, i=in_, f=func, kw=kw: e.activation(o, i, f, **kw), reads, writes)

    def tt(self, eng, out, in0, in1, op, reads, writes):
        self.sc.add(eng, lambda e, o=out, a=in0, b=in1, p=op: e.tensor_tensor(o, a, b, p), reads, writes)

    def ts(self, eng, out, in0, s1, s2, op0, op1, reads, writes):
        if s2 is None:
            fn = lambda e, o=out, a=in0, x=s1, p=op0: e.tensor_scalar(o, a, x, None, p)
        else:
            fn = lambda e, o=out, a=in0, x=s1, y=s2, p=op0, q=op1: e.tensor_scalar(o, a, x, y, p, q)
        self.sc.add(eng, fn, reads, writes)

    def stt(self, eng, out, in0, scalar, in1, op0, op1, reads, writes):
        self.sc.add(eng, lambda e, o=out, a=in0, s=scalar, b=in1, p=op0, q=op1:
                    e.scalar_tensor_tensor(o, a, s, b, p, q), reads, writes)

    def copy(self, eng, out, in_, reads, writes):
        if eng == "act":
            self.sc.add(eng, lambda e, o=out, i=in_: e.copy(o, i), reads, writes)
        else:
            self.sc.add(eng, lambda e, o=out, i=in_: e.tensor_copy(o, i), reads, writes)

    def recip(self, out, in_, reads, writes):
        self.sc.add("dve", lambda e, o=out, i=in_: e.reciprocal(o, i), reads, writes)

    def memset(self, eng, ap, val, writes):
        self.sc.add(eng, lambda e, a=ap, v=val: e.memset(a, v), (), writes)


def build(mode):
    nc = bass.Bass("TRN2", target_bir_lowering=False)
    doA = mode in ("AB", "A", "AX")
    doB = mode in ("AB", "B")
    fused = mode in ("AB", "AX")

    def din(name, shape, dt=F32):
        return nc.dram_tensor(name, list(shape), dt, kind="ExternalInput").ap()

    vecs_d = din("vecs", [128, NV])
    cf_d = din("cf32", [128, 256])
    if doA:
        xT_d = din("xT", [D, S])
        wqkv_d = din("wqkv", [D, 768])
    if doB:
        xTc_d = din("xTc", [D, HALO + TC])
        w_in_d = din("w_in", [D, 22528])
        w_ao_d = din("w_att_out", [2048, D])
        w_co_d = din("w_conv_out", [2048, D])
        w_out_d = din("w_out", [D, D])
        w_pg_d = din("w_ple_gate", [D, D])
        w_ple_d = din("w_ple", [256, D])
        pT_d = din("pT", [256, TC])
        outT_d = nc.dram_tensor("outT", [D, TC], F32, kind="ExternalOutput").ap()
    if fused:
        src_t = nc.dram_tensor("ex_src", [256, S], BF16)
        dst_t = nc.dram_tensor("ex_dst", [2048, S], BF16)
        src_d = src_t.ap()
        dst_d = dst_t.ap()
    elif doA:
        src_d = nc.dram_tensor("oT", [256, S], BF16, kind="ExternalOutput").ap()
    else:
        oTc_d = din("oTc", [2048, TC], BF16)

    sc = Sched()
    with contextlib.ExitStack() as stack:
        B = Builder(nc, stack, sc)
        vecs = B.sb("vecs_sb", [128, NV], F32)
        cf = B.sb("cf_sb", [128, 256], F32)
        ident_f = cf[:, 0:128]
        cbf = B.sb("cbf", [128, 384], BF16)
        ident_b = cbf[:, 0:128]
        tri_b = cbf[:, 128:256]
        ones_b = cbf[:, 256:384]
        small = B.sb("small", [128, 16], F32)
        eps_rms = small[:, 0:1]
        eps_sub = small[:, 1:2]
        eps_ln = small[:, 2:3]
        neg_lam = small[:, 3:4]
        gsub = B.sb("gsub", [128, 256], F32)
        NWS, NWB = 6, 6
        WS = B.sb("WS", [128, NWS * 512], F32)
        WB = B.sb("WB", [128, NWB * 512], BF16)
        banks = [stack.enter_context(nc.psum_tensor("B%d" % i, [128, 512], F32)) for i in range(8)]
        bk = lambda i: ("bank", i)

        B.dma(vecs[:, :], vecs_d[:, :], (), ["vecs"], "const", True)
        B.dma(cf[:, :], cf_d[:, :], (), ["cf"], "const", True)
        B.copy("dve", cbf[:, 0:256], cf[:, 0:256], ["cf"], ["cbf"])
        B.memset("dve", cbf[:, 256:384], 1.0, ["cbf1"])
        B.memset("dve", small[:, 0:1], RMS_EPS, ["small"])
        B.memset("dve", small[:, 1:2], SUBLN_EPS, ["small"])
        B.memset("dve", small[:, 2:3], LN_EPS, ["small"])
        CB = ["cbf", "cbf1"]

        CAST_ROT = ["act", "dve", "act", "dve", "act", "pool"]

        def wslab(src_ap, shape3=None):
            s = B.nxt("ws", NWS)
            r = B.nxt("wb", NWB)
            ceng = CAST_ROT[B.nxt("castrot", len(CAST_ROT))]
            ws = WS[:, s * 512:(s + 1) * 512]
            wb = WB[:, r * 512:(r + 1) * 512]
            if shape3 is not None:
                a, c = shape3
                wsv = ws[:, 0:a * c].rearrange("p (a c) -> p a c", a=a)
                B.dma(wsv, src_ap, (), [("WS", s)], ("ws", s))
                B.copy(ceng, wb[:, 0:a * c], ws[:, 0:a * c], [("WS", s)], [("WB", r)])
            else:
                n = src_ap.shape[1]
                B.dma(ws[:, 0:n], src_ap, (), [("WS", s)], ("ws", s))
                B.copy(ceng, wb[:, 0:n], ws[:, 0:n], [("WS", s)], [("WB", r)])
            return wb, ("WB", r)

        if doA:
            with contextlib.ExitStack() as sa:
                QT = [B.sb("QT%d" % c, [128, S], BF16, sa) for c in range(2)]
                KT = [B.sb("KT%d" % c, [128, S], BF16, sa) for c in range(2)]
                VE = B.sb("VE", [128, 64 * 257], BF16, sa)
                B.memset("pool", VE[:, :], 1.0, ["VE"])
                lt = B.sb("lam_t", [128, 8], F32, sa)
                ltmp = B.sb("lam_tmp", [128, 128], F32, sa)
                for i, (a, b_) in enumerate(((V_LQ1, V_LK1), (V_LQ2, V_LK2))):
                    B.tt("dve", ltmp[:, :], vecs[:, a:a + 128], vecs[:, b_:b_ + 128], ALU.mult, ["vecs"], ["ltmp"])
                    B.sc.add("dve", lambda e, o=lt[:, i:i + 1], x=ltmp[:, :]: e.reduce_sum(o, x, axis=AX.X), ["ltmp"], ["lt"])
                B.act(lt[:, 2:4], lt[:, 0:2], AF.Exp, ["lt"], ["lt"])
                B.tt("dve", lt[:, 4:5], lt[:, 3:4], lt[:, 2:3], ALU.subtract, ["lt"], ["lt"])
                B.ts("dve", neg_lam, lt[:, 4:5], -LAMBDA_INIT, None, ALU.add, None, ["lt"], ["neglam"])
                B.ts("dve", gsub[:, :], vecs[:, V_GSUB:V_GSUB + 256], 1.0 - LAMBDA_INIT, None, ALU.mult, None, ["vecs"], ["gsub"])

                with contextlib.ExitStack() as s1:
                    wq = B.sb("wq", [128, KC * 768], BF16, s1)
                    NXS = 4
                    XS = B.sb("XS", [128, NXS * 512], F32, s1)
                    XG = B.sb("XG", [128, NXS * 512], BF16, s1)
                    SQ = B.sb("SQ", [128, NXS * 512], BF16, s1)
                    rstd = B.sb("rstdA", [128, 512], F32, s1)
                    vT = B.sb("vT", [128, 1024], F32, s1)
                    for kc in range(KC):
                        for hf in range(2):
                            s = B.nxt("ws", NWS)
                            ws = WS[:, s * 512:s * 512 + 384]
                            B.dma(ws, wqkv_d[kc * 128:(kc + 1) * 128, hf * 384:(hf + 1) * 384], (), [("WS", s)], ("ws", s))
                            B.act(wq[:, kc * 768 + hf * 384: kc * 768 + (hf + 1) * 384], ws, AF.Identity, [("WS", s), "vecs"], [("wq", kc)],
                                  scale=vecs[:, V_GMIX + kc:V_GMIX + kc + 1])
                    for g in range(S // 512):
                        t0 = g * 512
                        for kc in range(KC):
                            s = B.nxt("xs", NXS)
                            xs = XS[:, s * 512:(s + 1) * 512]
                            xg = XG[:, s * 512:(s + 1) * 512]
                            sq = SQ[:, s * 512:(s + 1) * 512]
                            B.dma(xs, xT_d[kc * 128:(kc + 1) * 128, t0:t0 + 512], (), [("XS", s)], ("xs", s))
                            B.act(sq, xs, AF.Square, [("XS", s)], [("SQ", s)])
                            B.copy("pool" if kc % 4 == 3 else "dve", xg, xs, [("XS", s)], [("XG", s)])
                            for j in range(6):
                                B.mm(banks[j][:, :], wq[:, kc * 768 + j * 128: kc * 768 + (j + 1) * 128], xg,
                                     kc == 0, kc == KC - 1, [("XG", s), ("wq", kc)], [bk(j)])
                            B.mm(banks[6][:, :], ones_b, sq, kc == 0, kc == KC - 1, [("SQ", s)] + CB, [bk(6)])
                        B.act(rstd[:, :], banks[6][:, :], AF.Sqrt, [bk(6), "small"], [bk(6), "rstdA"], bias=eps_rms, scale=1.0 / D)
                        B.recip(rstd[:, :], rstd[:, :], ["rstdA"], ["rstdA"])
                        dsts = [QT[0], QT[1], KT[0], KT[1]]
                        for j in range(4):
                            B.tt("dve", dsts[j][:, t0:t0 + 512], banks[j][:, :], rstd[:, :], ALU.mult,
                                 [bk(j), "rstdA"], [bk(j), ("QK", j, g)])
                        for j in range(2):
                            B.tt("dve", vT[:, j * 512:(j + 1) * 512], banks[4 + j][:, :], rstd[:, :], ALU.mult,
                                 [bk(4 + j), "rstdA"], [bk(4 + j), "vT"])
                        for tt_ in range(4):
                            for dh in range(2):
                                bnk = 7 if tt_ < 2 else 6
                                col = (tt_ % 2) * 256 + dh * 128
                                B.tr(banks[bnk][:, col:col + 128], vT[:, dh * 512 + tt_ * 128: dh * 512 + (tt_ + 1) * 128], ident_f,
                                     ["vT", "cf"], [bk(bnk)])
                        for half in range(2):
                            bnk = 7 if half == 0 else 6
                            for t2 in range(2):
                                tk = 4 * g + half * 2 + t2
                                B.copy("act" if t2 else "dve", VE[:, tk * 257: tk * 257 + 256], banks[bnk][:, t2 * 256:(t2 + 1) * 256],
                                       [bk(bnk)], [bk(bnk), "VE"])
                barrier(B, sc, "a1a2")

                with contextlib.ExitStack() as s2:
                    NPT = 3
                    PT = B.sb("PT", [128, NPT * 512], BF16, s2)
                    OC = B.sb("OC", [128, 2 * 4 * 256], F32, s2)
                    RL = B.sb("RL", [128, 8], F32, s2)
                    OO = B.sb("OO", [128, 256], F32, s2)
                    O2 = B.sb("O2", [128, 256], F32, s2)
                    SS = B.sb("SS", [128, 4], F32, s2)
                    OT = B.sb("OTsb", [128, 2 * 1024], BF16, s2)
                    scale = 128.0 ** -0.5
                    for g in range(S // 512):
                        ob = g % 2
                        for c in range(2):
                            nk = 4 * g + 4
                            for kt in range(nk):
                                j = kt - 4 * g
                                q0 = 128 * max(j, 0)
                                N = 512 - q0
                                sb_ = B.nxt("st", 2)
                                st = banks[sb_]
                                B.mm(st[:, 0:N], KT[c][:, kt * 128:(kt + 1) * 128], QT[c][:, g * 512 + q0:(g + 1) * 512], True, True,
                                     [("QK", 2 + c, kt // 4), ("QK", c, g)], [bk(sb_)])
                                p = B.nxt("pt", NPT)
                                pt = PT[:, p * 512:(p + 1) * 512]
                                B.act(pt[:, 0:N], st[:, 0:N], AF.Exp, [bk(sb_)], [bk(sb_), ("PT", p)], scale=scale)
                                if j >= 0:
                                    B.tt("dve", pt[:, 0:128], pt[:, 0:128], tri_b, ALU.mult, [("PT", p)] + CB, [("PT", p)])
                                for qt in range(max(j, 0), 4):
                                    B.mm(banks[2 + qt][:, 0:257], pt[:, qt * 128 - q0: qt * 128 - q0 + 128], VE[:, kt * 257:(kt + 1) * 257],
                                         kt == 0, kt == 4 * g + qt, [("PT", p), "VE"], [bk(2 + qt)])
                            for qt in range(4):
                                acc = banks[2 + qt]
                                B.recip(RL[:, c * 4 + qt: c * 4 + qt + 1], acc[:, 256:257], [bk(2 + qt)], [bk(2 + qt), ("RL", c, qt)])
                                B.ts("dve", OC[:, (c * 4 + qt) * 256:(c * 4 + qt + 1) * 256], acc[:, 0:256], RL[:, c * 4 + qt: c * 4 + qt + 1], None,
                                     ALU.mult, None, [bk(2 + qt), ("RL", c, qt)], [bk(2 + qt), ("OC", c, qt)])
                        for qt in range(4):
                            o1 = OC[:, qt * 256:(qt + 1) * 256]
                            o2 = OC[:, (4 + qt) * 256:(5 + qt) * 256]
                            B.stt("dve", OO[:, :], o2, neg_lam, o1, ALU.mult, ALU.add, [("OC", 0, qt), ("OC", 1, qt), "neglam"], ["OO"])
                            B.tt("pool", O2[:, :], OO[:, :], OO[:, :], ALU.mult, ["OO"], ["O2"])
                            B.sc.add("dve", lambda e, o=SS[:, 0:1], x=O2[:, :]: e.reduce_sum(o, x, axis=AX.X), ["O2"], ["SS"])
                            B.act(SS[:, 1:2], SS[:, 0:1], AF.Sqrt, ["SS", "small"], ["SS1"], bias=eps_sub, scale=1.0 / 256)
                            B.recip(SS[:, 2:3], SS[:, 1:2], ["SS1"], ["SS2"])
                            B.stt("dve", OO[:, :], OO[:, :], SS[:, 2:3], gsub[:, :], ALU.mult, ALU.mult, ["OO", "SS2", "gsub"], ["OO"])
                            for dh in range(2):
                                B.tr(banks[6 + (qt % 2)][:, dh * 128:(dh + 1) * 128], OO[:, dh * 128:(dh + 1) * 128], ident_f,
                                     ["OO", "cf"], [bk(6 + (qt % 2))])
                            for dh in range(2):
                                B.copy("act" if dh else "dve", OT[:, ob * 1024 + dh * 512 + qt * 128: ob * 1024 + dh * 512 + (qt + 1) * 128],
                                       banks[6 + (qt % 2)][:, dh * 128:(dh + 1) * 128], [bk(6 + (qt % 2))], [bk(6 + (qt % 2)), ("OT", ob)])
                        B.dma(src_d.rearrange("(dh p) t -> p dh t", dh=2)[:, :, g * 512:(g + 1) * 512],
                              OT[:, ob * 1024:(ob + 1) * 1024].rearrange("p (dh t) -> p dh t", dh=2),
                              [("OT", ob)], ["src"], ("ot", ob))
                barrier(B, sc, "a2end", extra_reads=["src"])

        if fused:
            cc_sem = stack.enter_context(nc.semaphore("cc"))
            ccs = B.sb("ccs", [128, 2], F32)

            src2_t = nc.dram_tensor("ex_src2", [256, 1024], BF16)
            dst2_t = nc.dram_tensor("ex_dst2", [2048, 1024], BF16)

            def cc_fn(e):
                ins = e.collective_compute("AllGather", ALU.bypass, replica_groups=[list(range(NCORES))],
                                           ins=[src_t.ap().opt()], outs=[dst_t.ap().opt()])
                ins.then_inc(cc_sem, 1)
                e.wait_ge(cc_sem, 1)
                ins2 = e.collective_compute("AllGather", ALU.bypass, replica_groups=[list(range(NCORES))],
                                            ins=[src2_t.ap().opt()], outs=[dst2_t.ap().opt()])
                ins2.then_inc(cc_sem, 1)
                e.wait_ge(cc_sem, 2)
                return e.memset(ccs[:, 0:1], 0.0)
            sc.add("pool", cc_fn, ["src"], ["dst"])
            selI = B.sb("selI", [128, 8 * 128], BF16)
            for j in range(8):
                B.ts("dve", selI[:, j * 128:(j + 1) * 128], ident_b, vecs[:, V_SEL + j:V_SEL + j + 1], None, ALU.mult, None,
                     ["vecs"] + CB, [("selI", j)])

        if mode == "AX":
            dump_d = nc.dram_tensor("dump", [128, 8 * 512], BF16, kind="ExternalOutput").ap()
            dsb = B.sb("dsb", [128, 8 * 512], BF16)
            for j in range(8):
                B.dma(dsb[:, j * 512:(j + 1) * 512], dst_d[0:128, j * TC:j * TC + 512], ["dst"], [("dsb", j)], ("dsb", j))
                B.dma(dump_d[:, j * 512:(j + 1) * 512], dsb[:, j * 512:(j + 1) * 512], [("dsb", j)], ["dump"], ("dso", j))
        if doB:
            with contextlib.ExitStack() as sbk:
                R64 = B.sb("R64", [128, 16384], F32, sbk)
                R64b = R64[:, :].bitcast(BF16)
                rk = lambda i: ("R64", i)
                Xt = lambda j: (R64[:, j * 512:(j + 1) * 512], rk(j))
                COt = lambda ct: (R64[:, ct * 512:(ct + 1) * 512], rk(ct))
                AAt = lambda ct: (R64b[:, 16384 + ct * 512: 16384 + (ct + 1) * 512], rk(16 + ct // 2))
                CAt = lambda ct: (R64b[:, 24576 + ct * 512: 24576 + (ct + 1) * 512], rk(24 + ct // 2))
                XN = B.sb("XN", [128, KC * 512], BF16, sbk)
                XNt = lambda kc: (XN[:, kc * 512:(kc + 1) * 512], ("XN", kc))
                R2 = B.sb("R2", [128, 10368], F32, sbk)
                R2b = R2[:, :].bitcast(BF16)
                M = R2b[:, 0:KC * 512]
                Mt = lambda j: (M[:, j * 512:(j + 1) * 512], ("M", j))
                U = R2b[:, 0:16 * 544]
                DG = R2b[:, 8704:8704 + 2 * NTAP * 128]
                XH = R2[:, 8320:8320 + KC * HALO]
                XNH = R2b[:, 18688:18688 + KC * HALO]
                SQH = R2b[:, 19712:19712 + KC * HALO]
                UH = B.sb("UH", [128, 16 * HALO], BF16, sbk)
                NTB, NTF = 4, 3
                TMPB = B.sb("TMPB", [128, NTB * 512], BF16, sbk)
                TMPF = B.sb("TMPF", [128, NTF * 512], F32, sbk)
                SG = B.sb("SG", [128, 4 * 512], BF16, sbk)
                MA = B.sb("MA", [128, 4 * 512], F32, sbk)
                ST = B.sb("ST", [128, 2 * 512], F32, sbk)
                STA = ST[:, 0:512]
                STBs = ST[:, 512:1024]
                SMH = B.sb("SMH", [128, 128], F32, sbk)
                PB = B.sb("PB", [128, 1024], BF16, sbk)
                NOS = 2
                OST = B.sb("OST", [128, NOS * 512], F32, sbk)
                NOB = 4
                OBr = B.sb("OBr", [128, NOB * 512], BF16, sbk)

                def tmpb():
                    i = B.nxt("tmpb", NTB)
                    return TMPB[:, i * 512:(i + 1) * 512], ("TMPB", i)

                def tmpf3():
                    i = B.nxt("tmpf", NTF)
                    return TMPF[:, i * 512:(i + 1) * 512], ("TMPF", i), i

                def tmpf():
                    return tmpf3()[0:2]

                bank_rr = [0]

                def take_banks(n):
                    if n == 4:
                        s0 = bank_rr[0] % 2
                        bank_rr[0] += 1
                        return [4 * s0 + i for i in range(4)]
                    raise AssertionError

                def rstd_from(bank_i, dst, key, scale, eps_ap):
                    B.act(dst, banks[bank_i][:, :], AF.Sqrt, [bk(bank_i), "small"], [bk(bank_i), key], bias=eps_ap, scale=scale)
                    B.recip(dst, dst, [key], [key])

                def proj_group(wsrc_fn, nk, rhs_fn, nb=4):
                    bs = take_banks(4)
                    for k in range(nk):
                        wb, wkey = wslab(wsrc_fn(k))
                        rhs, rkey = rhs_fn(k)
                        for i in range(nb):
                            B.mm(banks[bs[i]][:, :], wb[:, i * 128:(i + 1) * 128], rhs, k == 0, k == nk - 1, [wkey, rkey], [bk(bs[i])])
                    return bs

                for b in range(NB):
                    c0 = HALO + b * TB
                    barrier(B, sc, "blk%d" % b)
                    for kc in range(KC):
                        xa, xk = Xt(kc)
                        B.dma(xa, xTc_d[kc * 128:(kc + 1) * 128, c0:c0 + TB], (), [xk], ("xl", kc))
                        tb_, tk_ = tmpb()
                        B.act(tb_, xa, AF.Square, [xk], [tk_])
                        B.mm(banks[0][:, :], ones_b, tb_, kc == 0, kc == KC - 1, [tk_] + CB, [bk(0)])
                    rstd_from(0, STA, "stA", 1.0 / D, eps_rms)
                    for kc in range(KC):
                        xa, xk = Xt(kc)
                        xn, xnk = XNt(kc)
                        B.stt("dve", xn, xa, vecs[:, V_GMIX + kc:V_GMIX + kc + 1], STA, ALU.mult, ALU.mult,
                              [xk, "vecs", "stA"], [xnk])
                    if b == 0:
                        for q4 in range(4):
                            B.dma(XH[:, q4 * 8 * HALO:(q4 + 1) * 8 * HALO].rearrange("p (k t) -> p k t", k=8),
                                  xTc_d[q4 * 1024:(q4 + 1) * 1024, 0:HALO].rearrange("(k p) t -> p k t", p=128),
                                  (), [("XH", q4)], ("xh", q4))
                        B.act(SQH[:, :], XH[:, :], AF.Square, [("XH", q) for q in range(4)], ["SQH"])
                        for kc in range(KC):
                            B.mm(banks[1][:, 0:HALO], ones_b, SQH[:, kc * HALO:(kc + 1) * HALO], kc == 0, kc == KC - 1, ["SQH"] + CB, [bk(1)])
                        B.act(SMH[:, 0:HALO], banks[1][:, 0:HALO], AF.Sqrt, [bk(1), "small"], [bk(1), "SMH"], bias=eps_rms, scale=1.0 / D)
                        B.recip(SMH[:, 0:HALO], SMH[:, 0:HALO], ["SMH"], ["SMH"])
                        for kc in range(KC):
                            B.stt("dve", XNH[:, kc * HALO:(kc + 1) * HALO], XH[:, kc * HALO:(kc + 1) * HALO],
                                  vecs[:, V_GMIX + kc:V_GMIX + kc + 1], SMH[:, 0:HALO], ALU.mult, ALU.mult,
                                  [("XH", kc // 8), "vecs", "SMH"], [("XNH", kc)])
                    else:
                        for ct in range(16):
                            B.copy("pool", U[:, ct * 544: ct * 544 + HALO], UH[:, ct * HALO:(ct + 1) * HALO], [("UH", ct)], [("U", ct)])

                    for gi in range(4):
                        bs = proj_group(lambda k, gi=gi: w_in_d[k * 128:(k + 1) * 128, C_ZA + gi * 512: C_ZA + (gi + 1) * 512], KC, XNt)
                        szs = []
                        for i in range(4):
                            tb_, tk_ = tmpb()
                            B.act(tb_, banks[bs[i]][:, :], AF.Silu, [bk(bs[i])], [bk(bs[i]), tk_])
                            szs.append((tb_, tk_))
                        for i in range(4):
                            ct = gi * 4 + i
                            aa, ak = AAt(ct)
                            tb_, tk_ = szs[i]
                            if fused:
                                sbk_ = 0 + (ct % 2)
                                bsel = bs[i]
                                for j in range(8):
                                    o_i = B.nxt("ob", NOB)
                                    ob_ = OBr[:, o_i * 512:(o_i + 1) * 512]
                                    B.dma(ob_, dst_d[ct * 128:(ct + 1) * 128, j * TC + b * TB: j * TC + (b + 1) * TB], ["dst"], [("OB", o_i)], ("ob", o_i))
                                    B.mm(banks[bsel][:, :], selI[:, j * 128:(j + 1) * 128], ob_, j == 0, j == 7, [("OB", o_i), ("selI", j)], [bk(bsel)])
                                B.tt("dve", aa, banks[bsel][:, :], tb_, ALU.mult, [bk(bsel), tk_], [bk(bsel), ak])
                            else:
                                o_i = B.nxt("ob", NOB)
                                ob_ = OBr[:, o_i * 512:(o_i + 1) * 512]
                                B.dma(ob_, oTc_d[ct * 128:(ct + 1) * 128, b * TB:(b + 1) * TB], (), [("OB", o_i)], ("ob", o_i))
                                B.tt("pool", aa, ob_, tb_, ALU.mult, [("OB", o_i), tk_], [ak])

                    SUMB, SQB = 6, 7
                    for ct in range(16):
                        set_ = ct % 2
                        bga, bgb, bh = 3 * set_, 3 * set_ + 1, 3 * set_ + 2
                        for k in range(KC):
                            wb, wkey = wslab(w_in_d[k * 128:(k + 1) * 128, C_GA:C_GA + 4096].rearrange("p (a c) -> p a c", a=2)[:, :, ct * 128:(ct + 1) * 128], (2, 128))
                            rhs, rkey = XNt(k)
                            B.mm(banks[bga][:, :], wb[:, 0:128], rhs, k == 0, k == KC - 1, [wkey, rkey], [bk(bga)])
                            B.mm(banks[bgb][:, :], wb[:, 128:256], rhs, k == 0, k == KC - 1, [wkey, rkey], [bk(bgb)])
                            if b == 0:
                                rh = XNH[:, k * HALO:(k + 1) * HALO]
                                B.mm(banks[bh][:, 0:HALO], wb[:, 0:128], rh, k == 0, k == KC - 1, [wkey, ("XNH", k)], [bk(bh)], skip=True)
                                B.mm(banks[bh][:, HALO:2 * HALO], wb[:, 128:256], rh, False, k == KC - 1, [wkey, ("XNH", k)], [bk(bh)], skip=True)
                        tf_, tfk = tmpf()
                        B.act(tf_, banks[bgb][:, :], AF.Sigmoid, [bk(bgb)], [bk(bgb), tfk])
                        B.tt("dve", U[:, ct * 544 + HALO: ct * 544 + 544], banks[bga][:, :], tf_, ALU.mult, [bk(bga), tfk], [bk(bga), ("U", ct)])
                        if b == 0:
                            B.act(SMH[:, 64:64 + HALO], banks[bh][:, HALO:2 * HALO], AF.Sigmoid, [bk(bh)], [bk(bh), "SMH2"])
                            B.tt("dve", U[:, ct * 544: ct * 544 + HALO], banks[bh][:, 0:HALO], SMH[:, 64:64 + HALO], ALU.mult,
                                 [bk(bh), "SMH2"], [bk(bh), ("U", ct)])
                        if b == 0:
                            B.copy("pool", UH[:, ct * HALO:(ct + 1) * HALO], U[:, ct * 544 + 512: ct * 544 + 544], [("U", ct)], [("UH", ct)])
                        dsl = ct % 2
                        for k in range(NTAP):
                            if k % 3 == 2:
                                B.ts("dve", DG[:, (dsl * NTAP + k) * 128:(dsl * NTAP + k + 1) * 128], ident_b,
                                     vecs[:, V_CW + ct * NTAP + k: V_CW + ct * NTAP + k + 1], None, ALU.mult, None,
                                     ["vecs"] + CB, [("DG", dsl)])
                            else:
                                B.act(DG[:, (dsl * NTAP + k) * 128:(dsl * NTAP + k + 1) * 128], ident_b, AF.Identity,
                                      ["vecs"] + CB, [("DG", dsl)], scale=vecs[:, V_CW + ct * NTAP + k: V_CW + ct * NTAP + k + 1])
                        cbk = bh
                        for k in range(NTAP):
                            B.mm(banks[cbk][:, :], DG[:, (dsl * NTAP + k) * 128:(dsl * NTAP + k + 1) * 128],
                                 U[:, ct * 544 + 2 + k: ct * 544 + 2 + k + 512], k == 0, k == NTAP - 1, [("DG", dsl), ("U", ct)], [bk(cbk)])
                        co, cok = COt(ct)
                        B.act(co, banks[cbk][:, :], AF.Identity, [bk(cbk), "vecs"], [bk(cbk), cok], bias=vecs[:, V_CB + ct:V_CB + ct + 1])
                        tb1, tk1 = tmpb()
                        B.copy("pool", tb1, co, [cok], [tk1])
                        tb2, tk2 = tmpb()
                        B.act(tb2, co, AF.Square, [cok], [tk2])
                        B.mm(banks[SUMB][:, :], ones_b, tb1, ct == 0, ct == 15, [tk1] + CB, [bk(SUMB)])
                        B.mm(banks[SQB][:, :], ones_b, tb2, ct == 0, ct == 15, [tk2] + CB, [bk(SQB)])
                    mean = STA
                    rstc = STBs
                    B.ts("dve", mean, banks[SUMB][:, :], 1.0 / 2048, None, ALU.mult, None, [bk(SUMB)], [bk(SUMB), "stA"])
                    tf_, tfk = tmpf()
                    B.tt("pool", tf_, mean, mean, ALU.mult, ["stA"], [tfk])
                    B.stt("dve", rstc, banks[SQB][:, :], 1.0 / 2048, tf_, ALU.mult, ALU.subtract, [bk(SQB), tfk], [bk(SQB), "stB"])
                    B.act(rstc, rstc, AF.Sqrt, ["stB", "small"], ["stB"], bias=eps_ln, scale=1.0)
                    B.recip(rstc, rstc, ["stB"], ["stB"])
                    for ct in range(16):
                        co, cok = COt(ct)
                        B.tt("dve" if ct % 2 else "pool", co, co, mean, ALU.subtract, [cok, "stA"], [cok])
                        B.tt("pool" if ct % 2 else "dve", co, co, rstc, ALU.mult, [cok, "stB"], [cok])
                        ca, cak = CAt(ct)
                        B.act(ca, co, AF.Silu, [cok, "vecs"], [cak], bias=vecs[:, V_LNB + ct:V_LNB + ct + 1], scale=vecs[:, V_LNG + ct:V_LNG + ct + 1])

                    for gi in range(4):
                        bs = proj_group(lambda k, gi=gi: w_in_d[k * 128:(k + 1) * 128, C_ZC + gi * 512: C_ZC + (gi + 1) * 512], KC, XNt)
                        for i in range(4):
                            ct = gi * 4 + i
                            tb_, tk_ = tmpb()
                            B.act(tb_, banks[bs[i]][:, :], AF.Silu, [bk(bs[i])], [bk(bs[i]), tk_])
                            ca, cak = CAt(ct)
                            B.tt("pool", ca, ca, tb_, ALU.mult, [cak, tk_], [cak])

                    barrier(B, sc, "s4_%d" % b)
                    for jj in range(8):
                        bs = proj_group(lambda k, jj=jj: w_in_d[k * 128:(k + 1) * 128, C_GTA + jj * 512: C_GTA + (jj + 1) * 512], KC, XNt)
                        for i in range(4):
                            B.act(SG[:, i * 512:(i + 1) * 512], banks[bs[i]][:, :], AF.Sigmoid, [bk(bs[i])], [bk(bs[i]), ("SG", i)])
                        bs = proj_group(lambda k, jj=jj: w_ao_d[k * 128:(k + 1) * 128, jj * 512:(jj + 1) * 512], 16, AAt)
                        for i in range(4):
                            B.tt("dve", MA[:, i * 512:(i + 1) * 512], banks[bs[i]][:, :], SG[:, i * 512:(i + 1) * 512], ALU.mult,
                                 [bk(bs[i]), ("SG", i)], [bk(bs[i]), ("MA", i)])
                        bs = proj_group(lambda k, jj=jj: w_in_d[k * 128:(k + 1) * 128, C_GTC + jj * 512: C_GTC + (jj + 1) * 512], KC, XNt)
                        for i in range(4):
                            B.act(SG[:, i * 512:(i + 1) * 512], banks[bs[i]][:, :], AF.Sigmoid, [bk(bs[i])], [bk(bs[i]), ("SG", i)])
                        bs = proj_group(lambda k, jj=jj: w_co_d[k * 128:(k + 1) * 128, jj * 512:(jj + 1) * 512], 16, CAt)
                        for i in range(4):
                            tf_, tfk = tmpf()
                            B.tt("dve", tf_, banks[bs[i]][:, :], SG[:, i * 512:(i + 1) * 512], ALU.mult,
                                 [bk(bs[i]), ("SG", i)], [bk(bs[i]), tfk])
                            m_, mk = Mt(jj * 4 + i)
                            B.tt("pool", m_, tf_, MA[:, i * 512:(i + 1) * 512], ALU.add, [tfk, ("MA", i)], [mk])

                    STB = 3
                    for jj in range(8):
                        bsu = [4, 5, 6, 7] if jj % 2 == 0 else [0, 1, 2, 7]
                        for k in range(KC):
                            wb, wkey = wslab(w_out_d[k * 128:(k + 1) * 128, jj * 512:(jj + 1) * 512])
                            rhs, rkey = Mt(k)
                            for i in range(4):
                                B.mm(banks[bsu[i]][:, :], wb[:, i * 128:(i + 1) * 128], rhs, k == 0, k == KC - 1, [wkey, rkey], [bk(bsu[i])])
                        for i in range(4):
                            j = jj * 4 + i
                            tf_, tfk, tfi = tmpf3()
                            B.dma(tf_, xTc_d[j * 128:(j + 1) * 128, c0:c0 + TB], (), [tfk], ("tfd", tfi))
                            xa, xk = Xt(j)
                            B.tt("dve", xa, banks[bsu[i]][:, :], tf_, ALU.add, [bk(bsu[i]), tfk], [bk(bsu[i]), xk])
                            tb_, tk_ = tmpb()
                            B.act(tb_, xa, AF.Square, [xk], [tk_])
                            B.mm(banks[STB][:, :], ones_b, tb_, j == 0, j == KC - 1, [tk_] + CB, [bk(STB)])
                    rstd_from(STB, STA, "stA", 1.0 / D, eps_rms)
                    for kc in range(KC):
                        xa, xk = Xt(kc)
                        xn, xnk = XNt(kc)
                        B.stt("dve", xn, xa, vecs[:, V_GPG + kc:V_GPG + kc + 1], STA, ALU.mult, ALU.mult,
                              [xk, "vecs", "stA"], [xnk])

                    for k in range(2):
                        tf_, tfk, tfi = tmpf3()
                        B.dma(tf_, pT_d[k * 128:(k + 1) * 128, b * TB:(b + 1) * TB], (), [tfk], ("tfd", tfi))
                        B.copy("pool", PB[:, k * 512:(k + 1) * 512], tf_, [tfk], [("PB", k)])
                    PBt = lambda k: (PB[:, k * 512:(k + 1) * 512], ("PB", k))
                    for jj in range(8):
                        bsu = [4, 5, 6, 7] if jj % 2 == 0 else [0, 1, 2, 7]
                        for k in range(2):
                            wb, wkey = wslab(w_ple_d[k * 128:(k + 1) * 128, jj * 512:(jj + 1) * 512])
                            rhs, rkey = PBt(k)
                            for i in range(4):
                                B.mm(banks[bsu[i]][:, :], wb[:, i * 128:(i + 1) * 128], rhs, k == 0, k == 1, [wkey, rkey], [bk(bsu[i])])
                        for i in range(4):
                            j = jj * 4 + i
                            m_, mk = Mt(j)
                            B.copy("dve", m_, banks[bsu[i]][:, :], [bk(bsu[i])], [bk(bsu[i]), mk])
                            tb_, tk_ = tmpb()
                            B.tt("pool", tb_, m_, m_, ALU.mult, [mk], [tk_])
                            B.mm(banks[STB][:, :], ones_b, tb_, j == 0, j == KC - 1, [tk_] + CB, [bk(STB)])
                    rstd_from(STB, STBs, "stB", 1.0 / D, eps_rms)

                    for jj in range(8):
                        bsu = [4, 5, 6, 7] if jj % 2 == 0 else [0, 1, 2, 7]
                        for k in range(KC):
                            wb, wkey = wslab(w_pg_d[k * 128:(k + 1) * 128, jj * 512:(jj + 1) * 512])
                            rhs, rkey = XNt(k)
                            for i in range(4):
                                B.mm(banks[bsu[i]][:, :], wb[:, i * 128:(i + 1) * 128], rhs, k == 0, k == KC - 1, [wkey, rkey], [bk(bsu[i])])
                        for i in range(4):
                            j = jj * 4 + i
                            tf_, tfk = tmpf()
                            B.act(tf_, banks[bsu[i]][:, :], AF.Sigmoid, [bk(bsu[i])], [bk(bsu[i]), tfk])
                            m_, mk = Mt(j)
                            B.tt("pool", tf_, tf_, m_, ALU.mult, [tfk, mk], [tfk])
                            B.stt("dve", tf_, tf_, vecs[:, V_GPP + j:V_GPP + j + 1], STBs, ALU.mult, ALU.mult, [tfk, "vecs", "stB"], [tfk])
                            xa, xk = Xt(j)
                            B.tt("pool", xa, xa, tf_, ALU.add, [xk, tfk], [xk])
                            tb_, tk_ = tmpb()
                            B.act(tb_, xa, AF.Square, [xk], [tk_])
                            B.mm(banks[STB][:, :], ones_b, tb_, j == 0, j == KC - 1, [tk_] + CB, [bk(STB)])
                    rstd_from(STB, STA, "stA", 1.0 / D, eps_rms)
                    for j in range(KC):
                        xa, xk = Xt(j)
                        o_i = B.nxt("ost", NOS)
                        os_ = OST[:, o_i * 512:(o_i + 1) * 512]
                        B.stt("dve", os_, xa, vecs[:, V_GFIN + j:V_GFIN + j + 1], STA, ALU.mult, ALU.mult,
                              [xk, "vecs", "stA"], [("OST", o_i)])
                        B.dma(outT_d[j * 128:(j + 1) * 128, b * TB:(b + 1) * TB], os_, [("OST", o_i)], ["outT"], ("ost", o_i))

        finals = [k for k in sc.dma_count if k[0] in ("ost", "ot", "dso")]
        sc.emit(nc, stack, final_keys=finals)
    return nc


def barrier(B, sc, name, extra_reads=()):
    engs = ["pe", "act", "dve", "pool", "sp"]
    last = {}
    for i, op in enumerate(sc.ops):
        if op["fn"] is not None:
            last[op["eng"]] = i
    for e in engs:
        idx = sc.add(e, None, list(extra_reads), [])
        sc.ops[idx]["deps"] = sorted(set(sc.ops[idx]["deps"]) | set(last.values()))


_cache = {}


def _prog(mode):
    if mode not in _cache:
        _cache[mode] = build(mode)
    return _cache[mode]


def _pack_vecs(inp, core):
    v = np.zeros((128, NV), np.float32)

    def cols(a, n):
        return np.ascontiguousarray(np.asarray(a, np.float32).reshape(n, 128).T)
    v[:, V_GMIX:V_GMIX + 32] = cols(inp["g_mix"][0], 32)
    v[:, V_GPG:V_GPG + 32] = cols(inp["g_ple_gate"][0], 32)
    v[:, V_GPP:V_GPP + 32] = cols(inp["g_ple_post"][0], 32)
    v[:, V_GFIN:V_GFIN + 32] = cols(inp["g_final"], 32)
    v[:, V_CB:V_CB + 16] = cols(inp["conv_b"][0], 16)
    v[:, V_LNG:V_LNG + 16] = cols(inp["ln_g"][0], 16)
    v[:, V_LNB:V_LNB + 16] = cols(inp["ln_b"][0], 16)
    cw = np.asarray(inp["conv_w"][0], np.float32)
    v[:, V_CW:V_CW + 16 * NTAP] = cw.T.reshape(16, 128, NTAP).transpose(1, 0, 2).reshape(128, 16 * NTAP)
    v[:, V_SEL + core] = 1.0
    for off, nm in ((V_LQ1, "lambda_q1"), (V_LK1, "lambda_k1"), (V_LQ2, "lambda_q2"), (V_LK2, "lambda_k2")):
        v[:, off:off + 128] = np.asarray(inp[nm][0], np.float32)[None, :]
    v[:, V_GSUB:V_GSUB + 256] = np.asarray(inp["g_subln"][0], np.float32)[None, :]
    return v


def kernel(**inp):
    x = np.asarray(inp["x"], np.float32)[0]
    xT = np.ascontiguousarray(x.T)
    w_in = np.asarray(inp["w_in"], np.float32)[0]
    cf = np.zeros((128, 256), np.float32)
    cf[:, 0:128] = np.eye(128, dtype=np.float32)
    cf[:, 128:256] = np.triu(np.ones((128, 128), np.float32))
    pT = np.ascontiguousarray(np.asarray(inp["p"], np.float32)[0, 0].T)
    mapsA, mapsB = [], []
    for c in range(NCORES):
        vec = _pack_vecs(inp, c)
        h = c
        wqkv = np.ascontiguousarray(np.concatenate(
            [w_in[:, 256 * h:256 * h + 256], w_in[:, 2048 + 256 * h:2048 + 256 * h + 256],
             w_in[:, 4096 + 256 * h:4096 + 256 * h + 256]], axis=1))
        mapsA.append({"vecs": vec, "cf32": cf, "xT": xT, "wqkv": wqkv})
        xTc = np.zeros((D, HALO + TC), np.float32)
        lo = c * TC - HALO
        if lo < 0:
            xTc[:, HALO:] = xT[:, 0:TC]
        else:
            xTc[:, :] = xT[:, lo:lo + HALO + TC]
        mapsB.append({"vecs": vec, "cf32": cf, "xTc": xTc, "w_in": w_in,
                      "w_att_out": np.asarray(inp["w_att_out"], np.float32)[0],
                      "w_conv_out": np.asarray(inp["w_conv_out"], np.float32)[0],
                      "w_out": np.asarray(inp["w_out"], np.float32)[0],
                      "w_ple_gate": np.asarray(inp["w_ple_gate"], np.float32)[0],
                      "w_ple": np.asarray(inp["w_ple"], np.float32)[0],
                      "pT": np.ascontiguousarray(pT[:, c * TC:(c + 1) * TC])})
    cores = list(range(NCORES))
    if FUSED:
        maps = [dict(mapsA[c], **mapsB[c]) for c in cores]
        res = run_bass_kernel_spmd(_prog("AB"), maps, core_ids=cores)
    else:
        resA = run_bass_kernel_spmd(_prog("A"), mapsA, core_ids=cores)
        oT = np.concatenate([np.asarray(resA.results[c]["oT"]) for c in cores], axis=0)
        for c in cores:
            mapsB[c]["oTc"] = np.ascontiguousarray(oT[:, c * TC:(c + 1) * TC])
        res = run_bass_kernel_spmd(_prog("B"), mapsB, core_ids=cores)
    outT = np.concatenate([np.asarray(res.results[c]["outT"], np.float32) for c in cores], axis=1)
    return np.ascontiguousarray(outT.T)[None, :, :].astype(np.float32)
```

```python
import contextlib
import numpy as np
import ml_dtypes
import concourse.bass as bass
import concourse.mybir as mybir
from concourse.bass_utils import run_bass_kernel_spmd

F32 = mybir.dt.float32
BF16 = mybir.dt.bfloat16
AF = mybir.ActivationFunctionType
ALU = mybir.AluOpType
AX = mybir.AxisListType

FUSED = False

NCORES = 8
S = 8192
D = 4096
KC = 32
TC = 1024
TB = 512
NB = 2
HALO = 32
C_ZA = 6144
C_GA = 8192
C_GB = 10240
C_ZC = 12288
C_GTA = 14336
C_GTC = 18432
NTAP = 31
LAMBDA_INIT = 0.2
RMS_EPS = 1e-6
SUBLN_EPS = 1e-5
LN_EPS = 1e-5

V_GMIX = 0
V_GPG = 32
V_GPP = 64
V_GFIN = 96
V_CB = 128
V_LNG = 144
V_LNB = 160
V_CW = 176
V_SEL = V_CW + 16 * NTAP
V_LQ1 = V_SEL + 8
V_LK1 = V_LQ1 + 128
V_LQ2 = V_LK1 + 128
V_LK2 = V_LQ2 + 128
V_GSUB = V_LK2 + 128
NV = V_GSUB + 256


class Sched:
    def __init__(self):
        self.ops = []
        self.last_w = {}
        self.readers = {}
        self.dma_count = {}
        self.group_all = set()

    def add(self, eng, fn, reads=(), writes=(), dma_key=None, group_all=False):
        idx = len(self.ops)
        deps = set()
        for k in reads:
            w = self.last_w.get(k)
            if w is not None:
                deps.add(w)
        for k in writes:
            w = self.last_w.get(k)
            if w is not None:
                deps.add(w)
            for r in self.readers.get(k, {}).values():
                deps.add(r)
        if dma_key is None and eng == "pe":
            deps = {d for d in deps if not (self.ops[d]["eng"] == "pe" and self.ops[d]["dma_key"] is None)}
        op = dict(eng=eng, fn=fn, deps=sorted(deps), dma_key=dma_key, signal=False, tick=None)
        if dma_key is not None:
            n = self.dma_count.get(dma_key, 0) + 1
            self.dma_count[dma_key] = n
            op["dma_ord"] = n
            if group_all:
                self.group_all.add(dma_key)
        self.ops.append(op)
        rid = eng if dma_key is None else ("dma", idx)
        for k in reads:
            self.readers.setdefault(k, {})[rid] = idx
        for k in writes:
            self.last_w[k] = idx
            self.readers[k] = {}
        return idx

    def emit(self, nc, stack, final_keys=()):
        ops = self.ops
        for op in ops:
            for d in op["deps"]:
                ops[d]["signal"] = True
        engs = ["pe", "act", "dve", "pool", "sp"]
        esem = {e: stack.enter_context(nc.semaphore("s_" + e)) for e in engs}
        dsem = {}
        for i, k in enumerate(self.dma_count):
            dsem[k] = stack.enter_context(nc.semaphore("d%d" % i))
        cnt = {e: 0 for e in engs}
        for op in ops:
            if op["dma_key"] is not None:
                k = op["dma_key"]
                if k in self.group_all:
                    op["tick"] = (dsem[k], 16 * self.dma_count[k])
                else:
                    op["tick"] = (dsem[k], 16 * op["dma_ord"])
            elif op["signal"]:
                cnt[op["eng"]] += 1
                op["tick"] = (esem[op["eng"]], cnt[op["eng"]])
        per = {e: [] for e in engs}
        for op in ops:
            per[op["eng"]].append(op)
        finals = [(dsem[k], 16 * self.dma_count[k]) for k in final_keys if k in dsem]

        def run(e, name):
            known = {}
            for op in per[name]:
                for d in op["deps"]:
                    sem, val = ops[d]["tick"]
                    if known.get(id(sem), 0) >= val:
                        continue
                    e.wait_ge(sem, val)
                    known[id(sem)] = val
                if op["fn"] is None:
                    assert not op["signal"]
                    continue
                ins = op["fn"](e)
                if op["dma_key"] is not None:
                    ins.then_inc(op["tick"][0], 16)
                elif op["signal"]:
                    ins.then_inc(op["tick"][0], 1)
            if name == "sp":
                for sem, val in finals:
                    e.wait_ge(sem, val)

        with nc.Block() as block:
            @block.sync
            def _(e):
                run(e, "sp")

            @block.tensor
            def _(e):
                run(e, "pe")

            @block.scalar
            def _(e):
                run(e, "act")

            @block.vector
            def _(e):
                run(e, "dve")

            @block.gpsimd
            def _(e):
                run(e, "pool")


class Builder:
    def __init__(self, nc, stack, sc):
        self.nc = nc
        self.stack = stack
        self.sc = sc
        self.rr = {}

    def sb(self, name, shape, dt, stack=None):
        return (stack or self.stack).enter_context(self.nc.sbuf_tensor(name, shape, dt))

    def nxt(self, name, n):
        i = self.rr.get(name, 0)
        self.rr[name] = i + 1
        return i % n

    def dma(self, out, in_, reads, writes, key, group_all=False):
        self.sc.add("sp", lambda e, o=out, i=in_: e.dma_start(out=o, in_=i), reads, writes,
                    dma_key=key, group_all=group_all)

    def mm(self, out, lhsT, rhs, start, stop, reads, writes, skip=False):
        if skip:
            fn = lambda e, o=out, l=lhsT, r=rhs, a=start, b=stop: e.matmul(o, l, r, start=a, stop=b, skip_group_check=True)
        else:
            fn = lambda e, o=out, l=lhsT, r=rhs, a=start, b=stop: e.matmul(o, l, r, start=a, stop=b)
        self.sc.add("pe", fn, reads, writes)

    def tr(self, out, in_, ident, reads, writes):
        self.sc.add("pe", lambda e, o=out, i=in_, d=ident: e.transpose(o, i, d), reads, writes)

    def act(self, out, in_, func, reads, writes, bias=None, scale=None):
        kw = {}
        if bias is not None:
            kw["bias"] = bias
        if scale is not None:
            kw["scale"] = scale
        self.sc.add("act", lambda e, o=out, i=in_, f=func, kw=kw: e.activation(o, i, f, **kw), reads, writes)

    def tt(self, eng, out, in0, in1, op, reads, writes):
        self.sc.add(eng, lambda e, o=out, a=in0, b=in1, p=op: e.tensor_tensor(o, a, b, p), reads, writes)

    def ts(self, eng, out, in0, s1, s2, op0, op1, reads, writes):
        if s2 is None:
            fn = lambda e, o=out, a=in0, x=s1, p=op0: e.tensor_scalar(o, a, x, None, p)
        else:
            fn = lambda e, o=out, a=in0, x=s1, y=s2, p=op0, q=op1: e.tensor_scalar(o, a, x, y, p, q)
        self.sc.add(eng, fn, reads, writes)

    def stt(self, eng, out, in0, scalar, in1, op0, op1, reads, writes):
        self.sc.add(eng, lambda e, o=out, a=in0, s=scalar, b=in1, p=op0, q=op1:
                    e.scalar_tensor_tensor(o, a, s, b, p, q), reads, writes)

    def copy(self, eng, out, in_, reads, writes):
        if eng == "act":
            self.sc.add(eng, lambda e, o=out, i=in_: e.copy(o, i), reads, writes)
        else:
            self.sc.add(eng, lambda e, o=out, i=in_: e.tensor_copy(o, i), reads, writes)

    def recip(self, out, in_, reads, writes):
        self.sc.add("dve", lambda e, o=out, i=in_: e.reciprocal(o, i), reads, writes)

    def memset(self, eng, ap, val, writes):
        self.sc.add(eng, lambda e, a=ap, v=val: e.memset(a, v), (), writes)


def build(mode):
    nc = bass.Bass("TRN2", target_bir_lowering=False)
    doA = mode in ("AB", "A", "AX")
    doB = mode in ("AB", "B")
    fused = mode in ("AB", "AX")

    def din(name, shape, dt=F32):
        return nc.dram_tensor(name, list(shape), dt, kind="ExternalInput").ap()

    vecs_d = din("vecs", [128, NV])
    cf_d = din("cf32", [128, 256])
    if doA:
        xT_d = din("xT", [D, S])
        wqkv_d = din("wqkv", [D, 768])
    if doB:
        xTc_d = din("xTc", [D, HALO + TC])
        w_in_d = din("w_in", [D, 22528])
        w_ao_d = din("w_att_out", [2048, D])
        w_co_d = din("w_conv_out", [2048, D])
        w_out_d = din("w_out", [D, D])
        w_pg_d = din("w_ple_gate", [D, D])
        w_ple_d = din("w_ple", [256, D])
        pT_d = din("pT", [256, TC])
        outT_d = nc.dram_tensor("outT", [D, TC], F32, kind="ExternalOutput").ap()
    if fused:
        src_t = nc.dram_tensor("ex_src", [256, S], BF16)
        dst_t = nc.dram_tensor("ex_dst", [2048, S], BF16)
        src_d = src_t.ap()
        dst_d = dst_t.ap()
    elif doA:
        src_d = nc.dram_tensor("oT", [256, S], BF16, kind="ExternalOutput").ap()
    else:
        oTc_d = din("oTc", [2048, TC], BF16)

    sc = Sched()
    with contextlib.ExitStack() as stack:
        B = Builder(nc, stack, sc)
        vecs = B.sb("vecs_sb", [128, NV], F32)
        cf = B.sb("cf_sb", [128, 256], F32)
        ident_f = cf[:, 0:128]
        cbf = B.sb("cbf", [128, 384], BF16)
        ident_b = cbf[:, 0:128]
        tri_b = cbf[:, 128:256]
        ones_b = cbf[:, 256:384]
        small = B.sb("small", [128, 16], F32)
        eps_rms = small[:, 0:1]
        eps_sub = small[:, 1:2]
        eps_ln = small[:, 2:3]
        neg_lam = small[:, 3:4]
        gsub = B.sb("gsub", [128, 256], F32)
        NWS, NWB = 6, 6
        WS = B.sb("WS", [128, NWS * 512], F32)
        WB = B.sb("WB", [128, NWB * 512], BF16)
        banks = [stack.enter_context(nc.psum_tensor("B%d" % i, [128, 512], F32)) for i in range(8)]
        bk = lambda i: ("bank", i)

        B.dma(vecs[:, :], vecs_d[:, :], (), ["vecs"], "const", True)
        B.dma(cf[:, :], cf_d[:, :], (), ["cf"], "const", True)
        B.copy("dve", cbf[:, 0:256], cf[:, 0:256], ["cf"], ["cbf"])
        B.memset("dve", cbf[:, 256:384], 1.0, ["cbf1"])
        B.memset("dve", small[:, 0:1], RMS_EPS, ["small"])
        B.memset("dve", small[:, 1:2], SUBLN_EPS, ["small"])
        B.memset("dve", small[:, 2:3], LN_EPS, ["small"])
        CB = ["cbf", "cbf1"]

        CAST_ROT = ["act", "dve", "act", "dve", "act", "pool"]

        def wslab(src_ap, shape3=None):
            s = B.nxt("ws", NWS)
            r = B.nxt("wb", NWB)
            ceng = CAST_ROT[B.nxt("castrot", len(CAST_ROT))]
            ws = WS[:, s * 512:(s + 1) * 512]
            wb = WB[:, r * 512:(r + 1) * 512]
            if shape3 is not None:
                a, c = shape3
                wsv = ws[:, 0:a * c].rearrange("p (a c) -> p a c", a=a)
                B.dma(wsv, src_ap, (), [("WS", s)], ("ws", s))
                B.copy(ceng, wb[:, 0:a * c], ws[:, 0:a * c], [("WS", s)], [("WB", r)])
            else:
                n = src_ap.shape[1]
                B.dma(ws[:, 0:n], src_ap, (), [("WS", s)], ("ws", s))
                B.copy(ceng, wb[:, 0:n], ws[:, 0:n], [("WS", s)], [("WB", r)])
            return wb, ("WB", r)

        if doA:
            with contextlib.ExitStack() as sa:
                QT = [B.sb("QT%d" % c, [128, S], BF16, sa) for c in range(2)]
                KT = [B.sb("KT%d" % c, [128, S], BF16, sa) for c in range(2)]
                VE = B.sb("VE", [128, 64 * 257], BF16, sa)
                B.memset("pool", VE[:, :], 1.0, ["VE"])
                lt = B.sb("lam_t", [128, 8], F32, sa)
                ltmp = B.sb("lam_tmp", [128, 128], F32, sa)
                for i, (a, b_) in enumerate(((V_LQ1, V_LK1), (V_LQ2, V_LK2))):
                    B.tt("dve", ltmp[:, :], vecs[:, a:a + 128], vecs[:, b_:b_ + 128], ALU.mult, ["vecs"], ["ltmp"])
                    B.sc.add("dve", lambda e, o=lt[:, i:i + 1], x=ltmp[:, :]: e.reduce_sum(o, x, axis=AX.X), ["ltmp"], ["lt"])
                B.act(lt[:, 2:4], lt[:, 0:2], AF.Exp, ["lt"], ["lt"])
                B.tt("dve", lt[:, 4:5], lt[:, 3:4], lt[:, 2:3], ALU.subtract, ["lt"], ["lt"])
                B.ts("dve", neg_lam, lt[:, 4:5], -LAMBDA_INIT, None, ALU.add, None, ["lt"], ["neglam"])
                B.ts("dve", gsub[:, :], vecs[:, V_GSUB:V_GSUB + 256], 1.0 - LAMBDA_INIT, None, ALU.mult, None, ["vecs"], ["gsub"])

                with contextlib.ExitStack() as s1:
                    wq = B.sb("wq", [128, KC * 768], BF16, s1)
                    NXS = 4
                    XS = B.sb("XS", [128, NXS * 512], F32, s1)
                    XG = B.sb("XG", [128, NXS * 512], BF16, s1)
                    SQ = B.sb("SQ", [128, NXS * 512], BF16, s1)
                    rstd = B.sb("rstdA", [128, 512], F32, s1)
                    vT = B.sb("vT", [128, 1024], F32, s1)
                    for kc in range(KC):
                        for hf in range(2):
                            s = B.nxt("ws", NWS)
                            ws = WS[:, s * 512:s * 512 + 384]
                            B.dma(ws, wqkv_d[kc * 128:(kc + 1) * 128, hf * 384:(hf + 1) * 384], (), [("WS", s)], ("ws", s))
                            B.act(wq[:, kc * 768 + hf * 384: kc * 768 + (hf + 1) * 384], ws, AF.Identity, [("WS", s), "vecs"], [("wq", kc)],
                                  scale=vecs[:, V_GMIX + kc:V_GMIX + kc + 1])
                    for g in range(S // 512):
                        t0 = g * 512
                        for kc in range(KC):
                            s = B.nxt("xs", NXS)
                            xs = XS[:, s * 512:(s + 1) * 512]
                            xg = XG[:, s * 512:(s + 1) * 512]
                            sq = SQ[:, s * 512:(s + 1) * 512]
                            B.dma(xs, xT_d[kc * 128:(kc + 1) * 128, t0:t0 + 512], (), [("XS", s)], ("xs", s))
                            B.act(sq, xs, AF.Square, [("XS", s)], [("SQ", s)])
                            B.copy("pool" if kc % 4 == 3 else "dve", xg, xs, [("XS", s)], [("XG", s)])
                            for j in range(6):
                                B.mm(banks[j][:, :], wq[:, kc * 768 + j * 128: kc * 768 + (j + 1) * 128], xg,
                                     kc == 0, kc == KC - 1, [("XG", s), ("wq", kc)], [bk(j)])
                            B.mm(banks[6][:, :], ones_b, sq, kc == 0, kc == KC - 1, [("SQ", s)] + CB, [bk(6)])
                        B.act(rstd[:, :], banks[6][:, :], AF.Sqrt, [bk(6), "small"], [bk(6), "rstdA"], bias=eps_rms, scale=1.0 / D)
                        B.recip(rstd[:, :], rstd[:, :], ["rstdA"], ["rstdA"])
                        dsts = [QT[0], QT[1], KT[0], KT[1]]
                        for j in range(4):
                            B.tt("dve", dsts[j][:, t0:t0 + 512], banks[j][:, :], rstd[:, :], ALU.mult,
                                 [bk(j), "rstdA"], [bk(j), ("QK", j, g)])
                        for j in range(2):
                            B.tt("dve", vT[:, j * 512:(j + 1) * 512], banks[4 + j][:, :], rstd[:, :], ALU.mult,
                                 [bk(4 + j), "rstdA"], [bk(4 + j), "vT"])
                        for tt_ in range(4):
                            for dh in range(2):
                                bnk = 7 if tt_ < 2 else 6
                                col = (tt_ % 2) * 256 + dh * 128
                                B.tr(banks[bnk][:, col:col + 128], vT[:, dh * 512 + tt_ * 128: dh * 512 + (tt_ + 1) * 128], ident_f,
                                     ["vT", "cf"], [bk(bnk)])
                        for half in range(2):
                            bnk = 7 if half == 0 else 6
                            for t2 in range(2):
                                tk = 4 * g + half * 2 + t2
                                B.copy("act" if t2 else "dve", VE[:, tk * 257: tk * 257 + 256], banks[bnk][:, t2 * 256:(t2 + 1) * 256],
                                       [bk(bnk)], [bk(bnk), "VE"])
                barrier(B, sc, "a1a2")

                with contextlib.ExitStack() as s2:
                    NPT = 4
                    PT = B.sb("PT", [128, NPT * 512], BF16, s2)
                    OC = B.sb("OC", [128, 2 * 4 * 256], F32, s2)
                    RL = B.sb("RL", [128, 8], F32, s2)
                    OO = B.sb("OO", [128, 256], F32, s2)
                    O2 = B.sb("O2", [128, 256], F32, s2)
                    SS = B.sb("SS", [128, 4], F32, s2)
                    OT = B.sb("OTsb", [128, 2 * 1024], BF16, s2)
                    scale = 128.0 ** -0.5
                    for g in range(S // 512):
                        ob = g % 2
                        for c in range(2):
                            nk = 4 * g + 4
                            LA = 2
                            stb = {}

                            def emit_qk(kt, g=g, c=c, stb=stb):
                                j = kt - 4 * g
                                q0 = 128 * max(j, 0)
                                N = 512 - q0
                                sb_ = (0, 1, 6)[B.nxt("st", 3)]
                                B.mm(banks[sb_][:, 0:N], KT[c][:, kt * 128:(kt + 1) * 128], QT[c][:, g * 512 + q0:(g + 1) * 512], True, True,
                                     [("QK", 2 + c, kt // 4), ("QK", c, g)], [bk(sb_)])
                                stb[kt] = sb_

                            for kt in range(min(LA, nk)):
                                emit_qk(kt)
                            for kt in range(nk):
                                if kt + LA < nk:
                                    emit_qk(kt + LA)
                                j = kt - 4 * g
                                q0 = 128 * max(j, 0)
                                N = 512 - q0
                                sb_ = stb[kt]
                                st = banks[sb_]
                                p = B.nxt("pt", NPT)
                                pt = PT[:, p * 512:(p + 1) * 512]
                                B.act(pt[:, 0:N], st[:, 0:N], AF.Exp, [bk(sb_)], [bk(sb_), ("PT", p)], scale=scale)
                                if j >= 0:
                                    B.tt("dve", pt[:, 0:128], pt[:, 0:128], tri_b, ALU.mult, [("PT", p)] + CB, [("PT", p)])
                                for qt in range(max(j, 0), 4):
                                    B.mm(banks[2 + qt][:, 0:257], pt[:, qt * 128 - q0: qt * 128 - q0 + 128], VE[:, kt * 257:(kt + 1) * 257],
                                         kt == 0, kt == 4 * g + qt, [("PT", p), "VE"], [bk(2 + qt)])
                            for qt in range(4):
                                acc = banks[2 + qt]
                                B.recip(RL[:, c * 4 + qt: c * 4 + qt + 1], acc[:, 256:257], [bk(2 + qt)], [bk(2 + qt), ("RL", c, qt)])
                                B.act(OC[:, (c * 4 + qt) * 256:(c * 4 + qt + 1) * 256], acc[:, 0:256], AF.Identity,
                                      [bk(2 + qt), ("RL", c, qt)], [bk(2 + qt), ("OC", c, qt)], scale=RL[:, c * 4 + qt: c * 4 + qt + 1])
                        for qt in range(4):
                            o1 = OC[:, qt * 256:(qt + 1) * 256]
                            o2 = OC[:, (4 + qt) * 256:(5 + qt) * 256]
                            B.stt("dve", OO[:, :], o2, neg_lam, o1, ALU.mult, ALU.add, [("OC", 0, qt), ("OC", 1, qt), "neglam"], ["OO"])
                            B.tt("pool", O2[:, :], OO[:, :], OO[:, :], ALU.mult, ["OO"], ["O2"])
                            B.sc.add("dve", lambda e, o=SS[:, 0:1], x=O2[:, :]: e.reduce_sum(o, x, axis=AX.X), ["O2"], ["SS"])
                            B.act(SS[:, 1:2], SS[:, 0:1], AF.Sqrt, ["SS", "small"], ["SS1"], bias=eps_sub, scale=1.0 / 256)
                            B.recip(SS[:, 2:3], SS[:, 1:2], ["SS1"], ["SS2"])
                            B.stt("dve", OO[:, :], OO[:, :], SS[:, 2:3], gsub[:, :], ALU.mult, ALU.mult, ["OO", "SS2", "gsub"], ["OO"])
                            for dh in range(2):
                                B.tr(banks[7][:, (qt % 2) * 256 + dh * 128:(qt % 2) * 256 + (dh + 1) * 128], OO[:, dh * 128:(dh + 1) * 128], ident_f,
                                     ["OO", "cf"], [bk(7)])
                            for dh in range(2):
                                B.copy("act" if dh else "dve", OT[:, ob * 1024 + dh * 512 + qt * 128: ob * 1024 + dh * 512 + (qt + 1) * 128],
                                       banks[7][:, (qt % 2) * 256 + dh * 128:(qt % 2) * 256 + (dh + 1) * 128], [bk(7)], [bk(7), ("OT", ob)])
                        B.dma(src_d.rearrange("(dh p) t -> p dh t", dh=2)[:, :, g * 512:(g + 1) * 512],
                              OT[:, ob * 1024:(ob + 1) * 1024].rearrange("p (dh t) -> p dh t", dh=2),
                              [("OT", ob)], ["src"], ("ot", ob))
                barrier(B, sc, "a2end", extra_reads=["src"])

        if fused:
            cc_sem = stack.enter_context(nc.semaphore("cc"))
            ccs = B.sb("ccs", [128, 2], F32)

            src2_t = nc.dram_tensor("ex_src2", [256, 1024], BF16)
            dst2_t = nc.dram_tensor("ex_dst2", [2048, 1024], BF16)

            def cc_fn(e):
                ins = e.collective_compute("AllGather", ALU.bypass, replica_groups=[list(range(NCORES))],
                                           ins=[src_t.ap().opt()], outs=[dst_t.ap().opt()])
                ins.then_inc(cc_sem, 1)
                e.wait_ge(cc_sem, 1)
                ins2 = e.collective_compute("AllGather", ALU.bypass, replica_groups=[list(range(NCORES))],
                                            ins=[src2_t.ap().opt()], outs=[dst2_t.ap().opt()])
                ins2.then_inc(cc_sem, 1)
                e.wait_ge(cc_sem, 2)
                return e.memset(ccs[:, 0:1], 0.0)
            sc.add("pool", cc_fn, ["src"], ["dst"])
            selI = B.sb("selI", [128, 8 * 128], BF16)
            for j in range(8):
                B.ts("dve", selI[:, j * 128:(j + 1) * 128], ident_b, vecs[:, V_SEL + j:V_SEL + j + 1], None, ALU.mult, None,
                     ["vecs"] + CB, [("selI", j)])

        if mode == "AX":
            dump_d = nc.dram_tensor("dump", [128, 8 * 512], BF16, kind="ExternalOutput").ap()
            dsb = B.sb("dsb", [128, 8 * 512], BF16)
            for j in range(8):
                B.dma(dsb[:, j * 512:(j + 1) * 512], dst_d[0:128, j * TC:j * TC + 512], ["dst"], [("dsb", j)], ("dsb", j))
                B.dma(dump_d[:, j * 512:(j + 1) * 512], dsb[:, j * 512:(j + 1) * 512], [("dsb", j)], ["dump"], ("dso", j))
        if doB:
            with contextlib.ExitStack() as sbk:
                R64 = B.sb("R64", [128, 16384], F32, sbk)
                R64b = R64[:, :].bitcast(BF16)
                rk = lambda i: ("R64", i)
                Xt = lambda j: (R64[:, j * 512:(j + 1) * 512], rk(j))
                COt = lambda ct: (R64[:, ct * 512:(ct + 1) * 512], rk(ct))
                AAt = lambda ct: (R64b[:, 16384 + ct * 512: 16384 + (ct + 1) * 512], rk(16 + ct // 2))
                CAt = lambda ct: (R64b[:, 24576 + ct * 512: 24576 + (ct + 1) * 512], rk(24 + ct // 2))
                XN = B.sb("XN", [128, KC * 512], BF16, sbk)
                XNt = lambda kc: (XN[:, kc * 512:(kc + 1) * 512], ("XN", kc))
                R2 = B.sb("R2", [128, 10368], F32, sbk)
                R2b = R2[:, :].bitcast(BF16)
                M = R2b[:, 0:KC * 512]
                Mt = lambda j: (M[:, j * 512:(j + 1) * 512], ("M", j))
                U = R2b[:, 0:16 * 544]
                DG = R2b[:, 8704:8704 + 2 * NTAP * 128]
                XH = R2[:, 8320:8320 + KC * HALO]
                XNH = R2b[:, 18688:18688 + KC * HALO]
                SQH = R2b[:, 19712:19712 + KC * HALO]
                UH = B.sb("UH", [128, 16 * HALO], BF16, sbk)
                NTB, NTF = 4, 3
                TMPB = B.sb("TMPB", [128, NTB * 512], BF16, sbk)
                TMPF = B.sb("TMPF", [128, NTF * 512], F32, sbk)
                SG = B.sb("SG", [128, 4 * 512], BF16, sbk)
                MA = B.sb("MA", [128, 4 * 512], F32, sbk)
                ST = B.sb("ST", [128, 2 * 512], F32, sbk)
                STA = ST[:, 0:512]
                STBs = ST[:, 512:1024]
                SMH = B.sb("SMH", [128, 128], F32, sbk)
                PB = B.sb("PB", [128, 1024], BF16, sbk)
                NOS = 2
                OST = B.sb("OST", [128, NOS * 512], F32, sbk)
                NOB = 4
                OBr = B.sb("OBr", [128, NOB * 512], BF16, sbk)

                def tmpb():
                    i = B.nxt("tmpb", NTB)
                    return TMPB[:, i * 512:(i + 1) * 512], ("TMPB", i)

                def tmpf3():
                    i = B.nxt("tmpf", NTF)
                    return TMPF[:, i * 512:(i + 1) * 512], ("TMPF", i), i

                def tmpf():
                    return tmpf3()[0:2]

                bank_rr = [0]

                def take_banks(n):
                    if n == 4:
                        s0 = bank_rr[0] % 2
                        bank_rr[0] += 1
                        return [4 * s0 + i for i in range(4)]
                    raise AssertionError

                def rstd_from(bank_i, dst, key, scale, eps_ap):
                    B.act(dst, banks[bank_i][:, :], AF.Sqrt, [bk(bank_i), "small"], [bk(bank_i), key], bias=eps_ap, scale=scale)
                    B.recip(dst, dst, [key], [key])

                def proj_group(wsrc_fn, nk, rhs_fn, nb=4):
                    bs = take_banks(4)
                    for k in range(nk):
                        wb, wkey = wslab(wsrc_fn(k))
                        rhs, rkey = rhs_fn(k)
                        for i in range(nb):
                            B.mm(banks[bs[i]][:, :], wb[:, i * 128:(i + 1) * 128], rhs, k == 0, k == nk - 1, [wkey, rkey], [bk(bs[i])])
                    return bs

                for b in range(NB):
                    c0 = HALO + b * TB
                    barrier(B, sc, "blk%d" % b)
                    for kc in range(KC):
                        xa, xk = Xt(kc)
                        B.dma(xa, xTc_d[kc * 128:(kc + 1) * 128, c0:c0 + TB], (), [xk], ("xl", kc))
                        tb_, tk_ = tmpb()
                        B.act(tb_, xa, AF.Square, [xk], [tk_])
                        B.mm(banks[0][:, :], ones_b, tb_, kc == 0, kc == KC - 1, [tk_] + CB, [bk(0)])
                    rstd_from(0, STA, "stA", 1.0 / D, eps_rms)
                    for kc in range(KC):
                        xa, xk = Xt(kc)
                        xn, xnk = XNt(kc)
                        B.stt("dve", xn, xa, vecs[:, V_GMIX + kc:V_GMIX + kc + 1], STA, ALU.mult, ALU.mult,
                              [xk, "vecs", "stA"], [xnk])
                    if b == 0:
                        for q4 in range(4):
                            B.dma(XH[:, q4 * 8 * HALO:(q4 + 1) * 8 * HALO].rearrange("p (k t) -> p k t", k=8),
                                  xTc_d[q4 * 1024:(q4 + 1) * 1024, 0:HALO].rearrange("(k p) t -> p k t", p=128),
                                  (), [("XH", q4)], ("xh", q4))
                        B.act(SQH[:, :], XH[:, :], AF.Square, [("XH", q) for q in range(4)], ["SQH"])
                        for kc in range(KC):
                            B.mm(banks[1][:, 0:HALO], ones_b, SQH[:, kc * HALO:(kc + 1) * HALO], kc == 0, kc == KC - 1, ["SQH"] + CB, [bk(1)])
                        B.act(SMH[:, 0:HALO], banks[1][:, 0:HALO], AF.Sqrt, [bk(1), "small"], [bk(1), "SMH"], bias=eps_rms, scale=1.0 / D)
                        B.recip(SMH[:, 0:HALO], SMH[:, 0:HALO], ["SMH"], ["SMH"])
                        for kc in range(KC):
                            B.stt("dve", XNH[:, kc * HALO:(kc + 1) * HALO], XH[:, kc * HALO:(kc + 1) * HALO],
                                  vecs[:, V_GMIX + kc:V_GMIX + kc + 1], SMH[:, 0:HALO], ALU.mult, ALU.mult,
                                  [("XH", kc // 8), "vecs", "SMH"], [("XNH", kc)])
                    else:
                        for ct in range(16):
                            B.copy("pool", U[:, ct * 544: ct * 544 + HALO], UH[:, ct * HALO:(ct + 1) * HALO], [("UH", ct)], [("U", ct)])

                    for gi in range(4):
                        bs = proj_group(lambda k, gi=gi: w_in_d[k * 128:(k + 1) * 128, C_ZA + gi * 512: C_ZA + (gi + 1) * 512], KC, XNt)
                        szs = []
                        for i in range(4):
                            tb_, tk_ = tmpb()
                            B.act(tb_, banks[bs[i]][:, :], AF.Silu, [bk(bs[i])], [bk(bs[i]), tk_])
                            szs.append((tb_, tk_))
                        for i in range(4):
                            ct = gi * 4 + i
                            aa, ak = AAt(ct)
                            tb_, tk_ = szs[i]
                            if fused:
                                sbk_ = 0 + (ct % 2)
                                bsel = bs[i]
                                for j in range(8):
                                    o_i = B.nxt("ob", NOB)
                                    ob_ = OBr[:, o_i * 512:(o_i + 1) * 512]
                                    B.dma(ob_, dst_d[ct * 128:(ct + 1) * 128, j * TC + b * TB: j * TC + (b + 1) * TB], ["dst"], [("OB", o_i)], ("ob", o_i))
                                    B.mm(banks[bsel][:, :], selI[:, j * 128:(j + 1) * 128], ob_, j == 0, j == 7, [("OB", o_i), ("selI", j)], [bk(bsel)])
                                B.tt("dve", aa, banks[bsel][:, :], tb_, ALU.mult, [bk(bsel), tk_], [bk(bsel), ak])
                            else:
                                o_i = B.nxt("ob", NOB)
                                ob_ = OBr[:, o_i * 512:(o_i + 1) * 512]
                                B.dma(ob_, oTc_d[ct * 128:(ct + 1) * 128, b * TB:(b + 1) * TB], (), [("OB", o_i)], ("ob", o_i))
                                B.tt("pool", aa, ob_, tb_, ALU.mult, [("OB", o_i), tk_], [ak])

                    SUMB, SQB = 6, 7
                    for ct in range(16):
                        set_ = ct % 2
                        bga, bgb, bh = 3 * set_, 3 * set_ + 1, 3 * set_ + 2
                        for k in range(KC):
                            wb, wkey = wslab(w_in_d[k * 128:(k + 1) * 128, C_GA:C_GA + 4096].rearrange("p (a c) -> p a c", a=2)[:, :, ct * 128:(ct + 1) * 128], (2, 128))
                            rhs, rkey = XNt(k)
                            B.mm(banks[bga][:, :], wb[:, 0:128], rhs, k == 0, k == KC - 1, [wkey, rkey], [bk(bga)])
                            B.mm(banks[bgb][:, :], wb[:, 128:256], rhs, k == 0, k == KC - 1, [wkey, rkey], [bk(bgb)])
                            if b == 0:
                                rh = XNH[:, k * HALO:(k + 1) * HALO]
                                B.mm(banks[bh][:, 0:HALO], wb[:, 0:128], rh, k == 0, k == KC - 1, [wkey, ("XNH", k)], [bk(bh)], skip=True)
                                B.mm(banks[bh][:, HALO:2 * HALO], wb[:, 128:256], rh, False, k == KC - 1, [wkey, ("XNH", k)], [bk(bh)], skip=True)
                        tf_, tfk = tmpf()
                        B.act(tf_, banks[bgb][:, :], AF.Sigmoid, [bk(bgb)], [bk(bgb), tfk])
                        B.tt("dve", U[:, ct * 544 + HALO: ct * 544 + 544], banks[bga][:, :], tf_, ALU.mult, [bk(bga), tfk], [bk(bga), ("U", ct)])
                        if b == 0:
                            B.act(SMH[:, 64:64 + HALO], banks[bh][:, HALO:2 * HALO], AF.Sigmoid, [bk(bh)], [bk(bh), "SMH2"])
                            B.tt("dve", U[:, ct * 544: ct * 544 + HALO], banks[bh][:, 0:HALO], SMH[:, 64:64 + HALO], ALU.mult,
                                 [bk(bh), "SMH2"], [bk(bh), ("U", ct)])
                        if b == 0:
                            B.copy("pool", UH[:, ct * HALO:(ct + 1) * HALO], U[:, ct * 544 + 512: ct * 544 + 544], [("U", ct)], [("UH", ct)])
                        dsl = ct % 2
                        for k in range(NTAP):
                            if k % 3 == 2:
                                B.ts("dve", DG[:, (dsl * NTAP + k) * 128:(dsl * NTAP + k + 1) * 128], ident_b,
                                     vecs[:, V_CW + ct * NTAP + k: V_CW + ct * NTAP + k + 1], None, ALU.mult, None,
                                     ["vecs"] + CB, [("DG", dsl)])
                            else:
                                B.act(DG[:, (dsl * NTAP + k) * 128:(dsl * NTAP + k + 1) * 128], ident_b, AF.Identity,
                                      ["vecs"] + CB, [("DG", dsl)], scale=vecs[:, V_CW + ct * NTAP + k: V_CW + ct * NTAP + k + 1])
                        cbk = bh
                        for k in range(NTAP):
                            B.mm(banks[cbk][:, :], DG[:, (dsl * NTAP + k) * 128:(dsl * NTAP + k + 1) * 128],
                                 U[:, ct * 544 + 2 + k: ct * 544 + 2 + k + 512], k == 0, k == NTAP - 1, [("DG", dsl), ("U", ct)], [bk(cbk)])
                        co, cok = COt(ct)
                        B.act(co, banks[cbk][:, :], AF.Identity, [bk(cbk), "vecs"], [bk(cbk), cok], bias=vecs[:, V_CB + ct:V_CB + ct + 1])
                        tb1, tk1 = tmpb()
                        B.copy("pool", tb1, co, [cok], [tk1])
                        tb2, tk2 = tmpb()
                        B.act(tb2, co, AF.Square, [cok], [tk2])
                        B.mm(banks[SUMB][:, :], ones_b, tb1, ct == 0, ct == 15, [tk1] + CB, [bk(SUMB)])
                        B.mm(banks[SQB][:, :], ones_b, tb2, ct == 0, ct == 15, [tk2] + CB, [bk(SQB)])
                    mean = STA
                    rstc = STBs
                    B.ts("dve", mean, banks[SUMB][:, :], 1.0 / 2048, None, ALU.mult, None, [bk(SUMB)], [bk(SUMB), "stA"])
                    tf_, tfk = tmpf()
                    B.tt("pool", tf_, mean, mean, ALU.mult, ["stA"], [tfk])
                    B.stt("dve", rstc, banks[SQB][:, :], 1.0 / 2048, tf_, ALU.mult, ALU.subtract, [bk(SQB), tfk], [bk(SQB), "stB"])
                    B.act(rstc, rstc, AF.Sqrt, ["stB", "small"], ["stB"], bias=eps_ln, scale=1.0)
                    B.recip(rstc, rstc, ["stB"], ["stB"])
                    for ct in range(16):
                        co, cok = COt(ct)
                        B.tt("dve" if ct % 2 else "pool", co, co, mean, ALU.subtract, [cok, "stA"], [cok])
                        B.tt("pool" if ct % 2 else "dve", co, co, rstc, ALU.mult, [cok, "stB"], [cok])
                        ca, cak = CAt(ct)
                        B.act(ca, co, AF.Silu, [cok, "vecs"], [cak], bias=vecs[:, V_LNB + ct:V_LNB + ct + 1], scale=vecs[:, V_LNG + ct:V_LNG + ct + 1])

                    for gi in range(4):
                        bs = proj_group(lambda k, gi=gi: w_in_d[k * 128:(k + 1) * 128, C_ZC + gi * 512: C_ZC + (gi + 1) * 512], KC, XNt)
                        for i in range(4):
                            ct = gi * 4 + i
                            tb_, tk_ = tmpb()
                            B.act(tb_, banks[bs[i]][:, :], AF.Silu, [bk(bs[i])], [bk(bs[i]), tk_])
                            ca, cak = CAt(ct)
                            B.tt("pool", ca, ca, tb_, ALU.mult, [cak, tk_], [cak])

                    barrier(B, sc, "s4_%d" % b)
                    for jj in range(8):
                        bs = proj_group(lambda k, jj=jj: w_in_d[k * 128:(k + 1) * 128, C_GTA + jj * 512: C_GTA + (jj + 1) * 512], KC, XNt)
                        for i in range(4):
                            B.act(SG[:, i * 512:(i + 1) * 512], banks[bs[i]][:, :], AF.Sigmoid, [bk(bs[i])], [bk(bs[i]), ("SG", i)])
                        bs = proj_group(lambda k, jj=jj: w_ao_d[k * 128:(k + 1) * 128, jj * 512:(jj + 1) * 512], 16, AAt)
                        for i in range(4):
                            B.tt("dve", MA[:, i * 512:(i + 1) * 512], banks[bs[i]][:, :], SG[:, i * 512:(i + 1) * 512], ALU.mult,
                                 [bk(bs[i]), ("SG", i)], [bk(bs[i]), ("MA", i)])
                        bs = proj_group(lambda k, jj=jj: w_in_d[k * 128:(k + 1) * 128, C_GTC + jj * 512: C_GTC + (jj + 1) * 512], KC, XNt)
                        for i in range(4):
                            B.act(SG[:, i * 512:(i + 1) * 512], banks[bs[i]][:, :], AF.Sigmoid, [bk(bs[i])], [bk(bs[i]), ("SG", i)])
                        bs = proj_group(lambda k, jj=jj: w_co_d[k * 128:(k + 1) * 128, jj * 512:(jj + 1) * 512], 16, CAt)
                        for i in range(4):
                            tf_, tfk = tmpf()
                            B.tt("dve", tf_, banks[bs[i]][:, :], SG[:, i * 512:(i + 1) * 512], ALU.mult,
                                 [bk(bs[i]), ("SG", i)], [bk(bs[i]), tfk])
                            m_, mk = Mt(jj * 4 + i)
                            B.tt("pool", m_, tf_, MA[:, i * 512:(i + 1) * 512], ALU.add, [tfk, ("MA", i)], [mk])

                    STB = 3
                    for jj in range(8):
                        bsu = [4, 5, 6, 7] if jj % 2 == 0 else [0, 1, 2, 7]
                        for k in range(KC):
                            wb, wkey = wslab(w_out_d[k * 128:(k + 1) * 128, jj * 512:(jj + 1) * 512])
                            rhs, rkey = Mt(k)
                            for i in range(4):
                                B.mm(banks[bsu[i]][:, :], wb[:, i * 128:(i + 1) * 128], rhs, k == 0, k == KC - 1, [wkey, rkey], [bk(bsu[i])])
                        for i in range(4):
                            j = jj * 4 + i
                            tf_, tfk, tfi = tmpf3()
                            B.dma(tf_, xTc_d[j * 128:(j + 1) * 128, c0:c0 + TB], (), [tfk], ("tfd", tfi))
                            xa, xk = Xt(j)
                            B.tt("dve", xa, banks[bsu[i]][:, :], tf_, ALU.add, [bk(bsu[i]), tfk], [bk(bsu[i]), xk])
                            tb_, tk_ = tmpb()
                            B.act(tb_, xa, AF.Square, [xk], [tk_])
                            B.mm(banks[STB][:, :], ones_b, tb_, j == 0, j == KC - 1, [tk_] + CB, [bk(STB)])
                    rstd_from(STB, STA, "stA", 1.0 / D, eps_rms)
                    for kc in range(KC):
                        xa, xk = Xt(kc)
                        xn, xnk = XNt(kc)
                        B.stt("dve", xn, xa, vecs[:, V_GPG + kc:V_GPG + kc + 1], STA, ALU.mult, ALU.mult,
                              [xk, "vecs", "stA"], [xnk])

                    for k in range(2):
                        tf_, tfk, tfi = tmpf3()
                        B.dma(tf_, pT_d[k * 128:(k + 1) * 128, b * TB:(b + 1) * TB], (), [tfk], ("tfd", tfi))
                        B.copy("pool", PB[:, k * 512:(k + 1) * 512], tf_, [tfk], [("PB", k)])
                    PBt = lambda k: (PB[:, k * 512:(k + 1) * 512], ("PB", k))
                    for jj in range(8):
                        bsu = [4, 5, 6, 7] if jj % 2 == 0 else [0, 1, 2, 7]
                        for k in range(2):
                            wb, wkey = wslab(w_ple_d[k * 128:(k + 1) * 128, jj * 512:(jj + 1) * 512])
                            rhs, rkey = PBt(k)
                            for i in range(4):
                                B.mm(banks[bsu[i]][:, :], wb[:, i * 128:(i + 1) * 128], rhs, k == 0, k == 1, [wkey, rkey], [bk(bsu[i])])
                        for i in range(4):
                            j = jj * 4 + i
                            m_, mk = Mt(j)
                            B.copy("dve", m_, banks[bsu[i]][:, :], [bk(bsu[i])], [bk(bsu[i]), mk])
                            tb_, tk_ = tmpb()
                            B.tt("pool", tb_, m_, m_, ALU.mult, [mk], [tk_])
                            B.mm(banks[STB][:, :], ones_b, tb_, j == 0, j == KC - 1, [tk_] + CB, [bk(STB)])
                    rstd_from(STB, STBs, "stB", 1.0 / D, eps_rms)

                    for jj in range(8):
                        bsu = [4, 5, 6, 7] if jj % 2 == 0 else [0, 1, 2, 7]
                        for k in range(KC):
                            wb, wkey = wslab(w_pg_d[k * 128:(k + 1) * 128, jj * 512:(jj + 1) * 512])
                            rhs, rkey = XNt(k)
                            for i in range(4):
                                B.mm(banks[bsu[i]][:, :], wb[:, i * 128:(i + 1) * 128], rhs, k == 0, k == KC - 1, [wkey, rkey], [bk(bsu[i])])
                        for i in range(4):
                            j = jj * 4 + i
                            tf_, tfk = tmpf()
                            B.act(tf_, banks[bsu[i]][:, :], AF.Sigmoid, [bk(bsu[i])], [bk(bsu[i]), tfk])
                            m_, mk = Mt(j)
                            B.tt("pool", tf_, tf_, m_, ALU.mult, [tfk, mk], [tfk])
                            B.stt("dve", tf_, tf_, vecs[:, V_GPP + j:V_GPP + j + 1], STBs, ALU.mult, ALU.mult, [tfk, "vecs", "stB"], [tfk])
                            xa, xk = Xt(j)
                            B.tt("pool", xa, xa, tf_, ALU.add, [xk, tfk], [xk])
                            tb_, tk_ = tmpb()
                            B.act(tb_, xa, AF.Square, [xk], [tk_])
                            B.mm(banks[STB][:, :], ones_b, tb_, j == 0, j == KC - 1, [tk_] + CB, [bk(STB)])
                    rstd_from(STB, STA, "stA", 1.0 / D, eps_rms)
                    for j in range(KC):
                        xa, xk = Xt(j)
                        o_i = B.nxt("ost", NOS)
                        os_ = OST[:, o_i * 512:(o_i + 1) * 512]
                        B.stt("dve", os_, xa, vecs[:, V_GFIN + j:V_GFIN + j + 1], STA, ALU.mult, ALU.mult,
                              [xk, "vecs", "stA"], [("OST", o_i)])
                        B.dma(outT_d[j * 128:(j + 1) * 128, b * TB:(b + 1) * TB], os_, [("OST", o_i)], ["outT"], ("ost", o_i))

        finals = [k for k in sc.dma_count if k[0] in ("ost", "ot", "dso")]
        sc.emit(nc, stack, final_keys=finals)
    return nc


def barrier(B, sc, name, extra_reads=()):
    engs = ["pe", "act", "dve", "pool", "sp"]
    last = {}
    for i, op in enumerate(sc.ops):
        if op["fn"] is not None:
            last[op["eng"]] = i
    for e in engs:
        idx = sc.add(e, None, list(extra_reads), [])
        sc.ops[idx]["deps"] = sorted(set(sc.ops[idx]["deps"]) | set(last.values()))


_cache = {}


def _prog(mode):
    if mode not in _cache:
        _cache[mode] = build(mode)
    return _cache[mode]


def _pack_vecs(inp, core):
    v = np.zeros((128, NV), np.float32)

    def cols(a, n):
        return np.ascontiguousarray(np.asarray(a, np.float32).reshape(n, 128).T)
    v[:, V_GMIX:V_GMIX + 32] = cols(inp["g_mix"][0], 32)
    v[:, V_GPG:V_GPG + 32] = cols(inp["g_ple_gate"][0], 32)
    v[:, V_GPP:V_GPP + 32] = cols(inp["g_ple_post"][0], 32)
    v[:, V_GFIN:V_GFIN + 32] = cols(inp["g_final"], 32)
    v[:, V_CB:V_CB + 16] = cols(inp["conv_b"][0], 16)
    v[:, V_LNG:V_LNG + 16] = cols(inp["ln_g"][0], 16)
    v[:, V_LNB:V_LNB + 16] = cols(inp["ln_b"][0], 16)
    cw = np.asarray(inp["conv_w"][0], np.float32)
    v[:, V_CW:V_CW + 16 * NTAP] = cw.T.reshape(16, 128, NTAP).transpose(1, 0, 2).reshape(128, 16 * NTAP)
    v[:, V_SEL + core] = 1.0
    for off, nm in ((V_LQ1, "lambda_q1"), (V_LK1, "lambda_k1"), (V_LQ2, "lambda_q2"), (V_LK2, "lambda_k2")):
        v[:, off:off + 128] = np.asarray(inp[nm][0], np.float32)[None, :]
    v[:, V_GSUB:V_GSUB + 256] = np.asarray(inp["g_subln"][0], np.float32)[None, :]
    return v


def kernel(**inp):
    x = np.asarray(inp["x"], np.float32)[0]
    xT = np.ascontiguousarray(x.T)
    w_in = np.asarray(inp["w_in"], np.float32)[0]
    cf = np.zeros((128, 256), np.float32)
    cf[:, 0:128] = np.eye(128, dtype=np.float32)
    cf[:, 128:256] = np.triu(np.ones((128, 128), np.float32))
    pT = np.ascontiguousarray(np.asarray(inp["p"], np.float32)[0, 0].T)
    mapsA, mapsB = [], []
    for c in range(NCORES):
        vec = _pack_vecs(inp, c)
        h = c
        wqkv = np.ascontiguousarray(np.concatenate(
            [w_in[:, 256 * h:256 * h + 256], w_in[:, 2048 + 256 * h:2048 + 256 * h + 256],
             w_in[:, 4096 + 256 * h:4096 + 256 * h + 256]], axis=1))
        mapsA.append({"vecs": vec, "cf32": cf, "xT": xT, "wqkv": wqkv})
        xTc = np.zeros((D, HALO + TC), np.float32)
        lo = c * TC - HALO
        if lo < 0:
            xTc[:, HALO:] = xT[:, 0:TC]
        else:
            xTc[:, :] = xT[:, lo:lo + HALO + TC]
        mapsB.append({"vecs": vec, "cf32": cf, "xTc": xTc, "w_in": w_in,
                      "w_att_out": np.asarray(inp["w_att_out"], np.float32)[0],
                      "w_conv_out": np.asarray(inp["w_conv_out"], np.float32)[0],
                      "w_out": np.asarray(inp["w_out"], np.float32)[0],
                      "w_ple_gate": np.asarray(inp["w_ple_gate"], np.float32)[0],
                      "w_ple": np.asarray(inp["w_ple"], np.float32)[0],
                      "pT": np.ascontiguousarray(pT[:, c * TC:(c + 1) * TC])})
    cores = list(range(NCORES))
    if FUSED:
        maps = [dict(mapsA[c], **mapsB[c]) for c in cores]
        res = run_bass_kernel_spmd(_prog("AB"), maps, core_ids=cores)
    else:
        resA = run_bass_kernel_spmd(_prog("A"), mapsA, core_ids=cores)
        oT = np.concatenate([np.asarray(resA.results[c]["oT"]) for c in cores], axis=0)
        for c in cores:
            mapsB[c]["oTc"] = np.ascontiguousarray(oT[:, c * TC:(c + 1) * TC])
        res = run_bass_kernel_spmd(_prog("B"), mapsB, core_ids=cores)
    outT = np.concatenate([np.asarray(res.results[c]["outT"], np.float32) for c in cores], axis=1)
    return np.ascontiguousarray(outT.T)[None, :, :].astype(np.float32)
```
